# Optimizing a Trainium2 kernel written in Bass

```python
import math
import jax, jax.numpy as jnp
from jax import lax
import numpy as np

D_MODEL = 2048
BATCH = 4
SEQ = 2048
DEPTH = 2
DEC_BATCH = 128
DEC_SEQ = 8
PAST_LEN = 16384
PAGE_SIZE = 128

S5_WIDTH = D_MODEL // 2
S5_GROUP = 16
S5_GROUPS = S5_WIDTH // S5_GROUP
S5_STATE = 64
LRU_WIDTH = D_MODEL // 2
LRU_BLOCKS = 8
LRU_BLOCK = LRU_WIDTH // LRU_BLOCKS
CONV_WIDTH = 4
LRU_C = 8.0
RET_HEADS = 8
RET_DK = 128
RET_DV = 128
RET_WIDTH = RET_HEADS * RET_DV
RET_CHUNK = 128
ROPE_BASE = 10000.0
NORM_EPS = 1e-6
GN_EPS = 1e-5
IN_SIZES = (S5_WIDTH, S5_WIDTH, LRU_WIDTH, LRU_WIDTH, RET_HEADS * RET_DK, RET_HEADS * RET_DK,
            RET_WIDTH, RET_WIDTH, D_MODEL, D_MODEL, D_MODEL)
N_IN = sum(IN_SIZES)

kernel_name = 'hybrid_s5_rglru_retention_step'


def _rmsnorm(x, g):
    xf = x.astype(jnp.float32)
    y = xf * lax.rsqrt(jnp.mean(xf * xf, axis=-1, keepdims=True) + NORM_EPS)
    return (y * g.astype(jnp.float32)).astype(x.dtype)


def _combine(e1, e2):
    a1, b1 = e1
    a2, b2 = e2
    return a1 * a2, a2 * b1 + b2


def _s5(u, h0_re, h0_im, lam_re, lam_im, log_dt, b_re, b_im, c_re, c_im, d, w_glu, b_glu):
    bsz, s = u.shape[0], u.shape[1]
    ug = u.reshape(bsz, s, S5_GROUPS, S5_GROUP)
    lam = lax.complex(lam_re.astype(jnp.float32), lam_im.astype(jnp.float32))
    dt = jnp.exp(log_dt.astype(jnp.float32))[:, None]
    lam_bar = jnp.exp(lam * dt)
    b = lax.complex(b_re.astype(jnp.float32), b_im.astype(jnp.float32))
    b_bar = ((lam_bar - 1.0) / lam)[..., None] * b
    bu = jnp.einsum('bsgc,gpc->bsgp', ug.astype(jnp.complex64), b_bar)
    h0 = lax.complex(h0_re.astype(jnp.float32), h0_im.astype(jnp.float32))
    bu = bu.at[:, 0].add(lam_bar * h0)
    a = jnp.broadcast_to(lam_bar, bu.shape)
    _, h = lax.associative_scan(_combine, (a, bu), axis=1)
    y = (jnp.einsum('bsgp,gcp->bsgc', h.real, c_re.astype(jnp.float32))
         - jnp.einsum('bsgp,gcp->bsgc', h.imag, c_im.astype(jnp.float32))
         + d.astype(jnp.float32).reshape(S5_GROUPS, S5_GROUP) * ug)
    y = jax.nn.gelu(y.reshape(bsz, s, S5_WIDTH))
    y = y * jax.nn.sigmoid(y @ w_glu.astype(jnp.float32) + b_glu.astype(jnp.float32))
    h_last = h[:, -1]
    return y, h_last.real, h_last.imag


def _rglru(u, h0, conv_buf, conv_w, conv_b, w_a, b_a, w_x, b_x, lam):
    bsz, s = u.shape[0], u.shape[1]
    xp = jnp.concatenate([conv_buf.astype(jnp.float32), u], axis=1)
    cw = conv_w.astype(jnp.float32)
    xc = sum(xp[:, k:k + s] * cw[k] for k in range(CONV_WIDTH)) + conv_b.astype(jnp.float32)
    new_buf = xp[:, -(CONV_WIDTH - 1):]
    xb = xc.reshape(bsz, s, LRU_BLOCKS, LRU_BLOCK)
    r = jax.nn.sigmoid(jnp.einsum('bsnc,ncd->bsnd', xb, w_a.astype(jnp.float32)).reshape(bsz, s, LRU_WIDTH)
                       + b_a.astype(jnp.float32))
    i = jax.nn.sigmoid(jnp.einsum('bsnc,ncd->bsnd', xb, w_x.astype(jnp.float32)).reshape(bsz, s, LRU_WIDTH)
                       + b_x.astype(jnp.float32))
    log_a = -LRU_C * r * jax.nn.softplus(-lam.astype(jnp.float32))
    a = jnp.exp(log_a)
    bx = jnp.sqrt(-jnp.expm1(2.0 * log_a)) * (i * xc)
    bx = bx.at[:, 0].add(a[:, 0] * h0.astype(jnp.float32))
    _, h = lax.associative_scan(_combine, (a, bx), axis=1)
    return h, h[:, -1], new_buf


def _rope(t, pos):
    half = t.shape[-1] // 2
    freq = ROPE_BASE ** (-jnp.arange(half, dtype=jnp.float32) / half)
    ang = pos[:, None] * freq[None, :]
    cos = jnp.cos(ang)[None, :, None, :]
    sin = jnp.sin(ang)[None, :, None, :]
    t1, t2 = t[..., :half], t[..., half:]
    return jnp.concatenate([t1 * cos - t2 * sin, t1 * sin + t2 * cos], axis=-1)


def _retention(q, k, v, r0, pos0, gn_g):
    bsz, s = q.shape[0], q.shape[1]
    pos = jnp.arange(s, dtype=jnp.float32) + float(pos0)
    q = _rope(q.reshape(bsz, s, RET_HEADS, RET_DK), pos)
    k = _rope(k.reshape(bsz, s, RET_HEADS, RET_DK), pos) * (RET_DK ** -0.5)
    v = v.reshape(bsz, s, RET_HEADS, RET_DV)
    c = math.gcd(s, RET_CHUNK)
    nc = s // c
    log_g = jnp.log1p(-jnp.exp2(-5.0 - jnp.arange(RET_HEADS, dtype=jnp.float32)))
    idx = jnp.arange(c, dtype=jnp.float32)
    diff = idx[:, None] - idx[None, :]
    dmask = jnp.where(diff[None] >= 0, jnp.exp(jnp.maximum(diff, 0.0)[None] * log_g[:, None, None]), 0.0)
    xi = jnp.exp((idx[:, None] + 1.0) * log_g[None, :])
    zeta = jnp.exp((c - 1.0 - idx[:, None]) * log_g[None, :])
    g_chunk = jnp.exp(c * log_g)[None, :, None, None]

    def to_chunks(t):
        return jnp.moveaxis(t.reshape(bsz, nc, c, RET_HEADS, t.shape[-1]), 1, 0)

    def step(r, inp):
        qc, kc, vc = inp
        sc = jnp.einsum('bihd,bjhd->bhij', qc, kc) * dmask[None]
        inner = jnp.einsum('bhij,bjhe->bihe', sc, vc)
        cross = jnp.einsum('bihd,bhde->bihe', qc, r) * xi[None, :, :, None]
        r_new = r * g_chunk + jnp.einsum('bjhd,bjhe->bhde', kc * zeta[None, :, :, None], vc)
        return r_new, inner + cross

    r_last, o = lax.scan(step, r0.astype(jnp.float32), (to_chunks(q), to_chunks(k), to_chunks(v)))
    o = jnp.moveaxis(o, 0, 1).reshape(bsz, s, RET_HEADS, RET_DV)
    mu = jnp.mean(o, axis=-1, keepdims=True)
    var = jnp.mean(jnp.square(o - mu), axis=-1, keepdims=True)
    o = ((o - mu) * lax.rsqrt(var + GN_EPS)).reshape(bsz, s, RET_WIDTH) * gn_g.astype(jnp.float32)
    return o, r_last


def _layer(x, pos0, h_s5_re, h_s5_im, h_lru, conv_buf, r_ret, lp):
    xn = _rmsnorm(x, lp['norm_g'])
    proj = (xn @ lp['w_in']).astype(jnp.float32)
    splits = np.cumsum(IN_SIZES)[:-1].tolist()
    u_s5, z_s5, u_lru, z_lru, q, k, v, z_ret, g_s5, g_lru, g_ret = jnp.split(proj, splits, axis=-1)
    y_s5, ns_re, ns_im = _s5(u_s5, h_s5_re, h_s5_im, lp['s5_lambda_re'], lp['s5_lambda_im'], lp['s5_log_dt'],
                             lp['s5_b_re'], lp['s5_b_im'], lp['s5_c_re'], lp['s5_c_im'], lp['s5_d'],
                             lp['s5_w_glu'], lp['s5_b_glu'])
    y_lru, n_lru, n_conv = _rglru(u_lru, h_lru, conv_buf, lp['lru_conv_w'], lp['lru_conv_b'], lp['lru_w_a'],
                                  lp['lru_b_a'], lp['lru_w_x'], lp['lru_b_x'], lp['lru_lambda'])
    y_ret, n_ret = _retention(q, k, v, r_ret, pos0, lp['ret_gn_g'])
    b_s5 = (y_s5 * jax.nn.silu(z_s5)) @ lp['w_branch_s5'].astype(jnp.float32)
    b_lru = (y_lru * jax.nn.silu(z_lru)) @ lp['w_branch_lru'].astype(jnp.float32)
    b_ret = (y_ret * jax.nn.silu(z_ret)) @ lp['w_branch_ret'].astype(jnp.float32)
    merged = jax.nn.sigmoid(g_s5) * b_s5 + jax.nn.sigmoid(g_lru) * b_lru + jax.nn.sigmoid(g_ret) * b_ret
    out = merged @ lp['w_out'].astype(jnp.float32)
    return (x + out.astype(x.dtype)), ns_re, ns_im, n_lru, n_conv, n_ret


def setup_inputs(seed: int = 0) -> dict:
    key = jax.random.key(seed)
    it = iter(jax.random.split(key, 48))
    nrm = lambda shape, scale: jax.random.normal(next(it), shape, jnp.float32) * scale
    L = DEPTH
    inp = {}
    inp['x_prompt'] = nrm((BATCH, SEQ, D_MODEL), 1.0)
    inp['x_sample'] = nrm((DEC_BATCH, DEC_SEQ, D_MODEL), 1.0)
    inp['state_s5_re'] = nrm((L, DEC_BATCH, S5_GROUPS, S5_STATE), 0.3)
    inp['state_s5_im'] = nrm((L, DEC_BATCH, S5_GROUPS, S5_STATE), 0.3)
    inp['state_lru'] = nrm((L, DEC_BATCH, LRU_WIDTH), 0.5)
    inp['state_conv'] = nrm((L, DEC_BATCH, CONV_WIDTH - 1, LRU_WIDTH), 1.0)
    inp['state_ret'] = nrm((L, DEC_BATCH, RET_HEADS, RET_DK, RET_DV), 0.3)
    inp['norm_g'] = 1.0 + nrm((L, D_MODEL), 0.02)
    inp['w_in'] = nrm((L, D_MODEL, N_IN), D_MODEL ** -0.5)
    inp['s5_lambda_re'] = -0.5 + nrm((L, S5_GROUPS, S5_STATE), 0.01)
    inp['s5_lambda_im'] = (jnp.pi * jnp.arange(S5_STATE, dtype=jnp.float32))[None, None, :] + nrm((L, S5_GROUPS, S5_STATE), 0.01)
    inp['s5_log_dt'] = jax.random.uniform(next(it), (L, S5_GROUPS), jnp.float32, math.log(1e-3), math.log(1e-1))
    inp['s5_b_re'] = nrm((L, S5_GROUPS, S5_STATE, S5_GROUP), (2 * S5_GROUP) ** -0.5)
    inp['s5_b_im'] = nrm((L, S5_GROUPS, S5_STATE, S5_GROUP), (2 * S5_GROUP) ** -0.5)
    inp['s5_c_re'] = nrm((L, S5_GROUPS, S5_GROUP, S5_STATE), S5_STATE ** -0.5)
    inp['s5_c_im'] = nrm((L, S5_GROUPS, S5_GROUP, S5_STATE), S5_STATE ** -0.5)
    inp['s5_d'] = nrm((L, S5_WIDTH), 1.0)
    inp['s5_w_glu'] = nrm((L, S5_WIDTH, S5_WIDTH), S5_WIDTH ** -0.5)
    inp['s5_b_glu'] = nrm((L, S5_WIDTH), 0.01)
    inp['lru_conv_w'] = nrm((L, CONV_WIDTH, LRU_WIDTH), CONV_WIDTH ** -0.5)
    inp['lru_conv_b'] = nrm((L, LRU_WIDTH), 0.01)
    inp['lru_w_a'] = nrm((L, LRU_BLOCKS, LRU_BLOCK, LRU_BLOCK), LRU_BLOCK ** -0.5)
    inp['lru_b_a'] = nrm((L, LRU_WIDTH), 0.01)
    inp['lru_w_x'] = nrm((L, LRU_BLOCKS, LRU_BLOCK, LRU_BLOCK), LRU_BLOCK ** -0.5)
    inp['lru_b_x'] = nrm((L, LRU_WIDTH), 0.01)
    a8 = jax.random.uniform(next(it), (L, LRU_WIDTH), jnp.float32, 0.9, 0.999)
    a_base = a8 ** (1.0 / LRU_C)
    inp['lru_lambda'] = jnp.log(a_base) - jnp.log1p(-a_base)
    inp['ret_gn_g'] = 1.0 + nrm((L, RET_WIDTH), 0.02)
    inp['w_branch_s5'] = nrm((L, S5_WIDTH, D_MODEL), S5_WIDTH ** -0.5)
    inp['w_branch_lru'] = nrm((L, LRU_WIDTH, D_MODEL), LRU_WIDTH ** -0.5)
    inp['w_branch_ret'] = nrm((L, RET_WIDTH, D_MODEL), RET_WIDTH ** -0.5)
    inp['w_out'] = nrm((L, D_MODEL, D_MODEL), D_MODEL ** -0.5)
    inp['final_norm_g'] = 1.0 + nrm((D_MODEL,), 0.02)
    return inp


def reference(x_prompt, x_sample, state_s5_re, state_s5_im, state_lru, state_conv, state_ret,
              norm_g, w_in, s5_lambda_re, s5_lambda_im, s5_log_dt, s5_b_re, s5_b_im, s5_c_re, s5_c_im,
              s5_d, s5_w_glu, s5_b_glu, lru_conv_w, lru_conv_b, lru_w_a, lru_b_a, lru_w_x, lru_b_x,
              lru_lambda, ret_gn_g, w_branch_s5, w_branch_lru, w_branch_ret, w_out, final_norm_g):
    f32 = jnp.float32
    bp = x_prompt.shape[0]
    xp, xs = x_prompt, x_sample
    new_p = [[] for _ in range(5)]
    new_s = [[] for _ in range(5)]
    for l in range(DEPTH):
        lp = dict(norm_g=norm_g[l], w_in=w_in[l], s5_lambda_re=s5_lambda_re[l], s5_lambda_im=s5_lambda_im[l],
                  s5_log_dt=s5_log_dt[l], s5_b_re=s5_b_re[l], s5_b_im=s5_b_im[l], s5_c_re=s5_c_re[l],
                  s5_c_im=s5_c_im[l], s5_d=s5_d[l], s5_w_glu=s5_w_glu[l], s5_b_glu=s5_b_glu[l],
                  lru_conv_w=lru_conv_w[l], lru_conv_b=lru_conv_b[l], lru_w_a=lru_w_a[l], lru_b_a=lru_b_a[l],
                  lru_w_x=lru_w_x[l], lru_b_x=lru_b_x[l], lru_lambda=lru_lambda[l], ret_gn_g=ret_gn_g[l],
                  w_branch_s5=w_branch_s5[l], w_branch_lru=w_branch_lru[l], w_branch_ret=w_branch_ret[l],
                  w_out=w_out[l])
        xp, *sp = _layer(xp, 0,
                         jnp.zeros((bp, S5_GROUPS, S5_STATE), f32), jnp.zeros((bp, S5_GROUPS, S5_STATE), f32),
                         jnp.zeros((bp, LRU_WIDTH), f32), jnp.zeros((bp, CONV_WIDTH - 1, LRU_WIDTH), f32),
                         jnp.zeros((bp, RET_HEADS, RET_DK, RET_DV), f32), lp)
        xs, *ss = _layer(xs, PAST_LEN, state_s5_re[l], state_s5_im[l], state_lru[l], state_conv[l],
                         state_ret[l], lp)
        for j in range(5):
            new_p[j].append(sp[j])
            new_s[j].append(ss[j])
    y_prompt = _rmsnorm(xp, final_norm_g)
    y_sample = _rmsnorm(xs, final_norm_g)
    s5_re_p, s5_im_p, lru_p, conv_p, ret_p = [jnp.stack(t, axis=0) for t in new_p]
    s5_re_s, s5_im_s, lru_s, conv_s, ret_s = [jnp.stack(t, axis=0) for t in new_s]
    return (y_prompt, y_sample, s5_re_p, s5_im_p, lru_p, conv_p, ret_p, s5_re_s, s5_im_s, lru_s, conv_s, ret_s)
```

```python
import numpy as np
import concourse.bass as bass
import concourse.mybir as mybir
from concourse.bass_utils import run_bass_kernel_spmd

F32 = mybir.dt.float32
BF16 = mybir.dt.bfloat16
AF = mybir.ActivationFunctionType
ALU = mybir.AluOpType
ESZ = {F32: 4, BF16: 2}

D = 2048; T = 256; NBLK_P = 8; NSEQ = 32; DEPTH = 2; NIN = 14336
EPOCH = 4096; NDS = 32
NCV = 120
O_NG, O_D, O_BGLU, O_CW, O_CB, O_BA, O_BX, O_LAM, O_GN, O_FNG = 0, 16, 24, 32, 64, 72, 80, 88, 96, 104
S5TW = 96 + 4 * 512
RET_G = [float(1.0 - 2.0 ** (-5.0 - h)) for h in range(8)]


class Sch:
    def __init__(self, nc):
        self.nc = nc
        self.eng = {'pe': nc.tensor, 'act': nc.scalar, 'dve': nc.vector, 'pool': nc.gpsimd, 'sp': nc.sync}
        self.cnt = {e: 0 for e in self.eng}
        self.sems = {e: [] for e in self.eng}
        self.seen = {e: {} for e in self.eng}
        self.recs = {}
        self.dsem = [nc.alloc_semaphore(name=f"dq{i}") for i in range(NDS)]
        self.dval = [0] * NDS
        self.dnext = 0
        self.dnext_sw = 0
        self.out_dmas = []
        self.nbank = 0
        self.banks = [nc.alloc_psum_tensor(f"bank{i}", [128, 512], F32) for i in range(8)]

    def bank(self):
        b = self.banks[self.nbank % 8]
        self.nbank += 1
        return b

    def esem(self, e, ep):
        while len(self.sems[e]) <= ep:
            self.sems[e].append(self.nc.alloc_semaphore(name=f"e_{e}_{len(self.sems[e])}"))
        return self.sems[e][ep]

    def iv(self, a):
        sp = str(a.space)
        name = a.tensor.name
        if 'DRAM' in sp.upper():
            if name.startswith('scr'):
                return (name, 0, 1 << 40)
            return None
        ap = a.ap
        pstep = ap[0][0]
        off = a.offset % pstep if pstep > 0 else a.offset
        lo = off; hi = off + 1
        for st, cn in ap[1:]:
            if st >= 0:
                hi += (cn - 1) * st
            else:
                lo += (cn - 1) * st
        es = ESZ[a.dtype]
        return (name, lo * es, hi * es)

    def _wait(self, e, tgt):
        if tgt[0] == 'E':
            _, te, n = tgt
            if self.seen[e].get(te, 0) >= n:
                return
            self.seen[e][te] = n
            self.eng[e].wait_ge(self.esem(te, (n - 1) // EPOCH), (n - 1) % EPOCH + 1)
        else:
            _, k, v = tgt
            if self.seen[e].get(('D', k), 0) >= v:
                return
            self.seen[e][('D', k)] = v
            self.eng[e].wait_ge(self.dsem[k], v)

    def _sync(self, e, R, W):
        for (name, lo, hi) in R:
            for r in self.recs.get(name, ()):
                if r[2] and r[0] < hi and lo < r[1]:
                    t = r[3]
                    if t[0] == 'E' and t[1] == e and e in ('pe',):
                        continue
                    self._wait(e, t)
        for (name, lo, hi) in W:
            for r in self.recs.get(name, ()):
                if r[0] < hi and lo < r[1]:
                    t = r[3]
                    if t[0] == 'E' and t[1] == e and e == 'pe':
                        continue
                    self._wait(e, t)

    def _record(self, R, W, tgt):
        for (name, lo, hi) in W:
            L = self.recs.setdefault(name, [])
            L[:] = [r for r in L if not (lo <= r[0] and r[1] <= hi)]
            L.append((lo, hi, True, tgt))
        for (name, lo, hi) in R:
            L = self.recs.setdefault(name, [])
            if tgt[0] == 'E':
                L[:] = [r for r in L if not ((not r[2]) and r[3][0] == 'E' and r[3][1] == tgt[1]
                                             and lo <= r[0] and r[1] <= hi)]
            L.append((lo, hi, False, tgt))

    def op(self, e, fn, reads=(), writes=()):
        R = [x for x in (self.iv(a) for a in reads) if x is not None]
        W = [x for x in (self.iv(a) for a in writes) if x is not None]
        self._sync(e, R, W)
        ins = fn()
        self.cnt[e] += 1
        n = self.cnt[e]
        ins.then_inc(self.esem(e, (n - 1) // EPOCH), 1)
        self._record(R, W, ('E', e, n))

    def dma(self, q, out, in_, is_output=False):
        R = [x for x in (self.iv(in_),) if x is not None]
        W = [x for x in (self.iv(out),) if x is not None]
        self._sync(q, R, W)
        if q == 'pool':
            k = NDS - 8 + (self.dnext_sw % 8); self.dnext_sw += 1
        else:
            k = self.dnext % (NDS - 8)
        self.dnext += 1
        if self.dval[k] > 0:
            self._wait(q, ('D', k, self.dval[k]))
        self.dval[k] += 16
        self.eng[q].dma_start(out=out, in_=in_).then_inc(self.dsem[k], 16)
        tgt = ('D', k, self.dval[k])
        self._record(R, W, tgt)
        if is_output:
            self.out_dmas.append(tgt)

    def finish(self):
        for k in range(NDS):
            if self.dval[k] > 0:
                self._wait('sp', ('D', k, self.dval[k]))
        for e in ('pe', 'act', 'dve', 'pool'):
            if self.cnt[e] > 0:
                self._wait('sp', ('E', e, self.cnt[e]))


def aps(*xs):
    return [x for x in xs if hasattr(x, 'ap') and hasattr(x, 'tensor')]


class B:
    def __init__(self):
        nc = bass.Bass("TRN2", target_bir_lowering=False)
        self.nc = nc
        self.s = Sch(nc)
        self.build()

    def mm(self, out, pairs):
        nc = self.nc
        reads = []
        for l, r in pairs:
            reads += [l, r]

        def fn():
            ins = None
            n = len(pairs)
            for i, (l, r) in enumerate(pairs):
                ins = nc.tensor.matmul(out, l, r, start=(i == 0), stop=(i == n - 1))
            return ins
        self.s.op('pe', fn, reads, [out])

    def mm_multi(self, items):
        nc = self.nc
        reads = []; writes = []
        for o, l, r in items:
            reads += [l, r]; writes.append(o)

        def fn():
            ins = None
            for o, l, r in items:
                ins = nc.tensor.matmul(o, l, r, start=True, stop=True)
            return ins
        self.s.op('pe', fn, reads, writes)

    def tr_multi(self, items, ident):
        nc = self.nc
        reads = [ident]; writes = []
        for o, i in items:
            reads.append(i); writes.append(o)

        def fn():
            ins = None
            for o, i in items:
                ins = nc.tensor.transpose(o, i, ident)
            return ins
        self.s.op('pe', fn, reads, writes)

    def act(self, out, in_, func, bias=0.0, scale=1.0):
        nc = self.nc
        self.s.op('act', lambda: nc.scalar.activation(out, in_, func, bias=bias, scale=scale),
                  aps(in_, bias, scale), [out])

    def E(self, e):
        return {'dve': self.nc.vector, 'pool': self.nc.gpsimd}[e]

    def tt(self, e, out, in0, in1, op):
        self.s.op(e, lambda: self.E(e).tensor_tensor(out, in0, in1, op), [in0, in1], [out])

    def ts(self, e, out, in0, s1, s2, op0, op1=None):
        if op1 is None:
            self.s.op(e, lambda: self.E(e).tensor_scalar(out, in0, s1, None, op0), aps(in0, s1), [out])
        else:
            self.s.op(e, lambda: self.E(e).tensor_scalar(out, in0, s1, s2, op0, op1), aps(in0, s1, s2), [out])

    def stt(self, e, out, in0, scalar, in1, op0, op1):
        self.s.op(e, lambda: self.E(e).scalar_tensor_tensor(out, in0, scalar, in1, op0, op1),
                  aps(in0, scalar, in1), [out])

    def cp(self, e, out, in_):
        if e == 'act':
            self.act(out, in_, AF.Copy)
        else:
            self.s.op(e, lambda: self.E(e).tensor_copy(out, in_), [in_], [out])

    def memset(self, e, out, val):
        self.s.op(e, lambda: self.E(e).memset(out, val), [], [out])

    def recip(self, out, in_):
        self.s.op('dve', lambda: self.nc.vector.reciprocal(out, in_), [in_], [out])

    def scan(self, out, d0, d1):
        self.s.op('dve', lambda: self.nc.vector.tensor_tensor_scan(out, d0, d1, 0.0, ALU.mult, ALU.add),
                  [d0, d1], [out])

    def dma(self, out, in_, q='sp', is_output=False):
        self.s.dma(q, out, in_, is_output)

    def av(self, off, shape, dt):
        n = int(np.prod(shape))
        nb = n * ESZ[dt]
        assert off % 4 == 0 and off + nb <= self.AB, (off, nb, self.AB)
        v = self.arena[:, off // 4: off // 4 + (nb + 3) // 4]
        if dt != F32:
            v = v.bitcast(dt)
        if len(shape) == 1:
            return v
        names = "abcd"[:len(shape)]
        kw = {names[i]: shape[i] for i in range(len(shape))}
        return v.rearrange("p (" + " ".join(names) + ") -> p " + " ".join(names), **kw)

    def tr32(self, dst, src, K, M, evac='act'):
        bk = self.s.bank()
        o = bk[0:M, 0:K]
        self.tr_multi([(o, src)], self.ident_f[0:K, 0:K])
        self.cp(evac, dst, o)

    def load_w(self, key, dram2d, rows, cols):
        slot = self.wslot % 4
        self.wslot += 1
        kt = rows // 128
        assert kt * cols <= 4096
        flat = self.wbuf[slot][:, 0: kt * cols]
        view = flat.rearrange("p (k c) -> p k c", k=kt, c=cols)
        if key not in self.scr:
            scr = self.nc.dram_tensor(f"scr_{len(self.scr)}", [128, kt * cols], BF16, kind="Internal").ap()
            self.scr[key] = scr
            src = dram2d.rearrange("(k p) c -> p k c", p=128)
            step = 8
            for k0 in range(0, kt, step):
                k1 = min(kt, k0 + step)
                self.dma(view[:, k0:k1, :], src[:, k0:k1, :], q='pool')
            self.dma(scr, flat, q='sp')
        else:
            self.dma(flat, self.scr[key], q='sp')
        return view

    def fm_chunk(self, w, kt, ncol_tiles, rhs_fn, evac):
        for m in range(ncol_tiles):
            bk = self.s.bank()
            o = bk[:, 0:T]
            self.mm(o, [(w[:, k, m * 128:(m + 1) * 128], rhs_fn(k)) for k in range(kt)])
            evac(m, o)

    def proj_fm(self, l, col0, ncols, evac):
        for c0 in range(0, ncols, 256):
            w = self.load_w((l, 'in', col0 + c0), self.w_in[l][:, col0 + c0: col0 + c0 + 256], 2048, 256)
            self.fm_chunk(w, 16, 2, lambda k: self.xnT[:, k, :], lambda m, o, c0=c0: evac(c0 // 128 + m, o))

    def build(self):
        nc = self.nc
        di = lambda name, shape: nc.dram_tensor(name, shape, F32, kind="ExternalInput").ap()
        do = lambda name, shape: nc.dram_tensor(name, shape, F32, kind="ExternalOutput").ap()
        self.xp = di("xp", [2048, D]); self.xs = di("xs", [256, D])
        self.st_s5re = di("st_s5re", [2, NSEQ, 4096]); self.st_s5im = di("st_s5im", [2, NSEQ, 4096])
        self.st_lru = di("st_lru", [2, NSEQ, 1024]); self.st_conv = di("st_conv", [2, NSEQ, 3, 1024])
        self.st_ret = di("st_ret", [2, NSEQ, 8, 128, 128])
        self.w_in = di("w_in", [2, D, NIN]); self.w_glu = di("w_glu", [2, 1024, 1024])
        self.w_a = di("w_a", [2, 1024, 128]); self.w_x = di("w_x", [2, 1024, 128])
        self.wb = [di("wb_s5", [2, 1024, D]), di("wb_lru", [2, 1024, D]), di("wb_ret", [2, 1024, D])]
        self.w_out = di("w_out", [2, D, D])
        self.cvec_d = di("cvec", [2, 128, NCV]); self.s5tab_d = di("s5tab", [2, 128, S5TW])
        self.cmat_d = di("cmat", [128, 3, 128])
        self.rope_p = di("rope_p", [128, 4, 2048]); self.rope_s = di("rope_s", [128, 4, 256])
        self.rmask_d = di("rmask", [2, 128, 2, 8, 128])
        self.rzeta_d = di("rzeta", [2, 128, 8]); self.bm_d = di("bm", [128, 16])
        self.yp = do("yp", [2048, D]); self.ys = do("ys", [256, D])
        self.o_s5_p = [do("o_s5re_p", [2, 4096]), do("o_s5im_p", [2, 4096])]
        self.o_lru_p = do("o_lru_p", [2, 1024]); self.o_conv_p = do("o_conv_p", [2, 3, 1024])
        self.o_ret_p = do("o_ret_p", [2, 8, 128, 128])
        self.o_s5_s = [do("o_s5re_s", [2, NSEQ, 4096]), do("o_s5im_s", [2, NSEQ, 4096])]
        self.o_lru_s = do("o_lru_s", [2, NSEQ, 1024]); self.o_conv_s = do("o_conv_s", [2, NSEQ, 3, 1024])
        self.o_ret_s = do("o_ret_s", [2, NSEQ, 8, 128, 128])

        sb = lambda name, shape, dt=F32: nc.alloc_sbuf_tensor("s_" + name, shape, dt)
        self.xT = sb("xT", [128, 16, T]); self.xnT = sb("xnT", [128, 16, T], BF16)
        self.merged = sb("merged", [128, 16, T], BF16)
        self.wbuf = [sb(f"wbuf{i}", [128, 4096], BF16) for i in range(4)]
        self.wslot = 0
        self.scr = {}
        self.cmat = sb("cmat", [128, 3, 128]); self.ident_f = self.cmat[:, 0, :]
        self.ones_f = self.cmat[:, 1, :]; self.swap_f = self.cmat[:, 2, :]
        self.ident_b = sb("ident_b", [128, 128], BF16); self.ones_b = sb("ones_b", [128, 128], BF16)
        self.cvec = sb("cvecs", [128, 2, NCV]); self.c1 = sb("c1", [128, 2, 8])
        self.rzeta = sb("rzeta", [128, 2, 8]); self.bm = sb("bm", [128, 16])
        self.buw = sb("buw", [128, 2 * 8 * 2 * 128], BF16)
        self.cw = sb("cw", [128, 2 * 32 * 64], BF16)
        self.s5w_scr = [nc.dram_tensor(f"scr_s5w{i}", [128, 8192], BF16, kind="Internal").ap() for i in range(2)]
        self.lamA = sb("lamA", [128, 2, 2, 32]); self.lamB = sb("lamB", [128, 2, 2, 32])
        self.lam8A = sb("lam8A", [128, 2, 2, 32]); self.lam8B = sb("lam8B", [128, 2, 2, 32])
        self.Zp = sb("Zp", [128, 2, 2, 32]); self.hst = sb("hst", [128, 2, 8]); self.convc = sb("convc", [128, 2, 8, 3])
        self.Rp = sb("Rp", [128, 2, 8, 128]); self.Rpb = sb("Rpb", [128, 2, 8, 128], BF16)
        self.AB = 104 * 1024
        self.arena = sb("arena", [128, self.AB // 4])

        self.dma(self.cmat[:, :, :], self.cmat_d)
        for l in range(2):
            self.dma(self.cvec[:, l, :], self.cvec_d[l])
            self.dma(self.rzeta[:, l, :], self.rzeta_d[l])
        self.dma(self.bm[:, :], self.bm_d)
        self.cp('dve', self.ident_b[:, :], self.ident_f)
        self.cp('dve', self.ones_b[:, :], self.ones_f)
        for t_ in (self.Zp, self.hst, self.convc, self.Rp, self.Rpb):
            shp = t_.shape
            v = t_[:, :, :, :] if len(shp) == 4 else t_[:, :, :]
            self.memset('dve', v, 0.0)
        for l in range(2):
            self.precompute(l)

        for blk in range(NBLK_P + 1):
            sample = (blk == NBLK_P)
            self.load_x(blk, sample)
            for l in range(2):
                self.layer(l, blk, sample)
            self.final_out(blk, sample)
            if blk == NBLK_P - 1:
                self.prompt_state_out()
        self.s.finish()

    def precompute(self, l):
        A = self
        cv = self.cvec[:, l, :]
        t0 = self.av(0, [8], F32)
        self.act(t0, cv[:, O_LAM:O_LAM + 8], AF.Exp, scale=-1.0)
        self.act(t0, t0, AF.Ln, bias=1.0)
        self.ts('dve', self.c1[:, l, :], t0, -8.0, None, ALU.mult)
        tab = self.av(1024, [S5TW], F32)
        self.dma(tab, self.s5tab_d[l])
        lamre = tab[:, 0:32]; lamim = tab[:, 32:64]; logdt = tab[:, 64:96]
        tb = lambda i: tab[:, 96 + i * 512: 96 + (i + 1) * 512].rearrange("p (a b) -> p a b", a=32, b=16)
        Bre, Bim, Cre, Cim = tb(0), tb(1), tb(2), tb(3)
        w = [self.av(12 * 1024 + i * 128, [32], F32) for i in range(16)]
        dt_, a_, b_, mag, u1, sn, cs, lre, lim, nre, den, wre, wim, x1, x2, x3 = w
        PI = float(np.pi)
        self.act(dt_, logdt, AF.Exp)
        self.tt('dve', a_, lamre, dt_, ALU.mult)
        self.tt('dve', b_, lamim, dt_, ALU.mult)
        self.act(mag, a_, AF.Exp)
        MAGIC = 12582912.0
        for dst, sh in ((sn, 0.0), (cs, 0.25)):
            self.ts('dve', u1, b_, 1.0 / (2 * PI), sh, ALU.mult, ALU.add)
            self.ts('dve', x1, u1, MAGIC, None, ALU.add)
            self.ts('dve', x1, x1, -MAGIC, None, ALU.add)
            self.tt('dve', u1, u1, x1, ALU.subtract)
            self.act(dst, u1, AF.Sin, scale=2 * PI)
        self.tt('dve', lre, mag, cs, ALU.mult)
        self.tt('dve', lim, mag, sn, ALU.mult)
        for c in range(2):
            self.cp('dve', self.lamA[:, l, c, :], lre)
        self.ts('dve', self.lamB[:, l, 0, :], lim, -1.0, None, ALU.mult)
        self.cp('dve', self.lamB[:, l, 1, :], lim)
        pr, pi_ = x1, x2
        self.cp('dve', pr, lre); self.cp('dve', pi_, lim)
        for _ in range(3):
            self.tt('dve', x3, pr, pr, ALU.mult)
            self.tt('dve', den, pi_, pi_, ALU.mult)
            self.tt('dve', pi_, pr, pi_, ALU.mult)
            self.ts('dve', pi_, pi_, 2.0, None, ALU.mult)
            self.tt('dve', pr, x3, den, ALU.subtract)
        for c in range(2):
            self.cp('dve', self.lam8A[:, l, c, :], pr)
        self.ts('dve', self.lam8B[:, l, 0, :], pi_, -1.0, None, ALU.mult)
        self.cp('dve', self.lam8B[:, l, 1, :], pi_)
        self.ts('dve', nre, lre, -1.0, None, ALU.add)
        self.tt('dve', x1, lamre, lamre, ALU.mult)
        self.tt('dve', x2, lamim, lamim, ALU.mult)
        self.tt('dve', den, x1, x2, ALU.add)
        self.recip(den, den)
        self.tt('dve', x1, nre, lamre, ALU.mult)
        self.tt('dve', x2, lim, lamim, ALU.mult)
        self.tt('dve', x3, x1, x2, ALU.add)
        self.tt('dve', wre, x3, den, ALU.mult)
        self.tt('dve', x1, lim, lamre, ALU.mult)
        self.tt('dve', x2, nre, lamim, ALU.mult)
        self.tt('dve', x3, x1, x2, ALU.subtract)
        self.tt('dve', wim, x3, den, ALU.mult)
        bb = [self.av(16 * 1024 + i * 2048, [32, 16], F32) for i in range(4)]
        bbre, bbim, y1, y2 = bb
        bc = lambda v: v.unsqueeze(2).broadcast_to([128, 32, 16])
        self.tt('dve', y1, Bre, bc(wre), ALU.mult)
        self.tt('dve', y2, Bim, bc(wim), ALU.mult)
        self.tt('dve', bbre, y1, y2, ALU.subtract)
        self.tt('dve', y1, Bim, bc(wre), ALU.mult)
        self.tt('dve', y2, Bre, bc(wim), ALU.mult)
        self.tt('dve', bbim, y1, y2, ALU.add)
        M1 = self.av(32 * 1024, [8, 128], F32)
        for c, src in ((0, bbre), (1, bbim)):
            for v in range(2):
                self.memset('dve', M1, 0.0)
                for pm in (v, v + 2):
                    for h in range(2):
                        col = 32 * pm + 16 * h
                        self.cp('dve', M1[h * 64:(h + 1) * 64, :, col:col + 16],
                                src[h * 64:(h + 1) * 64, pm::4, :])
                for ct in range(8):
                    base = ((c * 8 + ct) * 2 + v) * 128
                    self.tr32(self.buw[:, base:base + 128], M1[:, ct, :], 128, 128)
        for c, src, sg in ((0, Cre, 1.0), (1, Cim, -1.0)):
            base = c * 32 * 64
            cwv = self.cw[:, base: base + 32 * 64].rearrange("p (a b) -> p a b", a=32, b=64)
            self.memset('dve', cwv, 0.0)
            for pm2 in range(2):
                for h in range(2):
                    col = pm2 * 32 + h * 16
                    self.ts('dve', cwv[h * 64:(h + 1) * 64, pm2::2, col:col + 16],
                            src[h * 64:(h + 1) * 64, pm2::2, :], sg, None, ALU.mult)

        self.dma(self.s5w_scr[l][:, 0:4096], self.buw[:, :])
        self.dma(self.s5w_scr[l][:, 4096:8192], self.cw[:, :])

    def load_x(self, blk, sample):
        src = self.xs if sample else self.xp[blk * T:(blk + 1) * T, :]
        stg = self.av(0, [2, D], F32)
        for tt_ in range(2):
            self.dma(stg[:, tt_, :], src[tt_ * 128:(tt_ + 1) * 128, :])
        for tt_ in range(2):
            for k0 in range(0, 16, 4):
                bk = self.s.bank()
                items = [(bk[:, j * 128:(j + 1) * 128], stg[:, tt_, (k0 + j) * 128:(k0 + j + 1) * 128]) for j in range(4)]
                self.tr_multi(items, self.ident_f)
                self.cp('act' if (k0 // 4) % 2 else 'dve', self.xT[:, k0:k0 + 4, tt_ * 128:(tt_ + 1) * 128],
                        bk[:, :].rearrange("p (a b) -> p a b", a=4, b=128))

    def rmsnorm(self, gcols, out_fn):
        xsq = self.av(0, [16, T], BF16)
        self.act(xsq, self.xT[:, :, :], AF.Square)
        bk = self.s.bank()
        self.mm(bk[:, 0:T], [(self.ones_b[:, :], xsq[:, k, :]) for k in range(16)])
        rstd = self.av(8192, [T], F32)
        self.act(rstd, bk[:, 0:T], AF.Sqrt, bias=1e-6, scale=1.0 / D)
        self.recip(rstd, rstd)
        for k in range(16):
            self.stt('dve', out_fn(k), self.xT[:, k, :], gcols[:, k:k + 1], rstd, ALU.mult, ALU.mult)

    def final_out(self, blk, sample):
        yf = self.av(16 * 1024, [16, T], F32)
        self.rmsnorm(self.cvec[:, 0, O_FNG:O_FNG + 16], lambda k: yf[:, k, :])
        dst = self.ys if sample else self.yp[blk * T:(blk + 1) * T, :]
        stg = self.av(32 * 1024, [2, D], F32)
        for tt_ in range(2):
            for k0 in range(0, 16, 4):
                bk = self.s.bank()
                items = [(bk[:, j * 128:(j + 1) * 128], yf[:, k0 + j, tt_ * 128:(tt_ + 1) * 128]) for j in range(4)]
                self.tr_multi(items, self.ident_f)
                self.cp('act' if (k0 // 4) % 2 else 'dve', stg[:, tt_, k0 * 128:(k0 + 4) * 128], bk[:, :])
            self.dma(dst[tt_ * 128:(tt_ + 1) * 128, :], stg[:, tt_, :], is_output=True)

    def bg_gates(self, l):
        for m in range(3):
            for c0 in range(0, 2048, 256):
                col = 8192 + m * 2048 + c0
                w = self.load_w((l, 'in', col), self.w_in[l][:, col: col + 256], 2048, 256)
                for mt in range(2):
                    bk = self.s.bank()
                    o = bk[:, 0:T]
                    self.mm(o, [(w[:, k, mt * 128:(mt + 1) * 128], self.xnT[:, k, :]) for k in range(16)])
                    self.act(self.sg[m][:, c0 // 128 + mt, :], o, AF.Sigmoid)
                    self.bg_done += 1
                    yield

    def gen_v(self, l, vtok):
        for c0 in range(0, 1024, 256):
            w = self.load_w((l, 'in', 6144 + c0), self.w_in[l][:, 6144 + c0: 6144 + c0 + 256], 2048, 256)
            for tt_ in range(2):
                bk = self.s.bank()
                self.mm(bk[:, 0:256], [(self.xnT[:, k, tt_ * 128:(tt_ + 1) * 128], w[:, k, :]) for k in range(16)])
                self.cp('act', vtok[:, tt_, c0:c0 + 256], bk[:, 0:256])
                yield

    def gen_silu_proj(self, l, col0, dst):
        for c0 in range(0, 1024, 256):
            w = self.load_w((l, 'in', col0 + c0), self.w_in[l][:, col0 + c0: col0 + c0 + 256], 2048, 256)
            for mt in range(2):
                bk = self.s.bank()
                o = bk[:, 0:T]
                self.mm(o, [(w[:, k, mt * 128:(mt + 1) * 128], self.xnT[:, k, :]) for k in range(16)])
                self.act(dst[:, c0 // 128 + mt, :], o, AF.Silu)
                yield

    def bg_all(self, l):
        KB = 1024
        for _ in self.bg_gates(l):
            yield
        for g in (self.gen_silu_proj(l, 3072, self.av(36 * KB, [8, T], BF16)),
                  self.gen_v(l, self.av(40 * KB, [2, 1024], BF16)),
                  self.gen_silu_proj(l, 7168, self.av(44 * KB, [8, T], BF16))):
            for _ in g:
                self.bg_done += 1
                yield

    def drain(self, n):
        while self.bg is not None and self.bg_done < n:
            self.tick()

    def tick(self, n=1):
        for _ in range(n):
            if self.bg is None:
                return
            try:
                next(self.bg)
            except StopIteration:
                self.bg = None

    def gate_branch(self, l, m, ysb):
        tmp = self.av(28 * 1024, [T], F32)
        tmp2 = self.av(29 * 1024, [T], F32)
        use_bg = self.sg is not None
        if use_bg:
            self.drain(16 * (m + 1))
        else:
            sgt = self.av(96 * 1024, [4, T], F32)
        for dc in range(4):
            if not use_bg:
                def ev_g(j, o):
                    self.act(sgt[:, j % 4, :], o, AF.Sigmoid)
                self.proj_fm(l, 8192 + m * 2048 + dc * 512, 512, ev_g)
            w = self.load_w((l, 'wb', m, dc), self.wb[m][l][:, dc * 512:(dc + 1) * 512], 1024, 512)

            def ev_b(j, o, dc=dc):
                dt_ = dc * 4 + j
                g = self.sg[m][:, dt_, :] if use_bg else sgt[:, j, :]
                if m == 0:
                    self.tt('dve', self.merged[:, dt_, :], g, o, ALU.mult)
                else:
                    tm = tmp if dt_ % 2 == 0 else tmp2
                    self.tt('dve', tm, g, o, ALU.mult)
                    self.tt('dve', self.merged[:, dt_, :], self.merged[:, dt_, :], tm, ALU.add)
            self.fm_chunk(w, 8, 4, lambda k: ysb[:, k, :], ev_b)

    def layer(self, l, blk, sample):
        self.dma(self.buw[:, :], self.s5w_scr[l][:, 0:4096])
        self.dma(self.cw[:, :], self.s5w_scr[l][:, 4096:8192])
        self.rmsnorm(self.cvec[:, l, O_NG:O_NG + 16], lambda k: self.xnT[:, k, :])
        ysb = self.av(72 * 1024, [8, T], BF16)
        self.sg = [self.av((80 + 8 * m) * 1024, [16, T], BF16) for m in range(3)]
        self.bg_done = 0
        self.bg = self.bg_all(l)
        self.s5(l, blk, sample, ysb)
        g_lru = self.lru(l, blk, sample, ysb)
        next(g_lru)
        self.gate_branch(l, 0, ysb)
        for _ in g_lru:
            pass
        g_ret = self.ret(l, blk, sample, ysb)
        next(g_ret)
        self.gate_branch(l, 1, ysb)
        for _ in g_ret:
            pass
        self.gate_branch(l, 2, ysb)
        for dc in range(8):
            w = self.load_w((l, 'out', dc), self.w_out[l][:, dc * 256:(dc + 1) * 256], 2048, 256)

            def ev_o(j, o, dc=dc):
                dt_ = dc * 2 + j
                self.tt('dve', self.xT[:, dt_, :], self.xT[:, dt_, :], o, ALU.add)
            self.fm_chunk(w, 16, 2, lambda k: self.merged[:, k, :], ev_o)

    def cstep(self, out, prev, add, A, Bsw, t1, t2):
        self.tt('dve', t1, prev, A, ALU.mult)
        self.tt('dve', t2, prev[:, :, ::-1, :], Bsw, ALU.mult)
        self.tt('dve', t1, t1, t2, ALU.add)
        self.tt('dve', out, add, t1, ALU.add)
        self.tick()

    def s5(self, l, blk, sample, ysb):
        cv = self.cvec[:, l, :]
        KB = 1024
        BUH = self.av(0, [128, 2, 32], F32)
        hbs = [self.av((32 + 2 * j) * KB, [128, 2, 4], BF16) for j in range(2)]
        uT = self.av(48 * KB, [8, T], BF16)
        sz = self.av(52 * KB, [8, T], BF16)
        yT = self.av(56 * KB, [8, T], F32)
        nsq, J = (16, 1) if sample else (1, 16)
        NJ = nsq * J
        t1 = self.av(64 * KB, [NJ, 2, 32], F32)
        t2 = self.av(68 * KB, [NJ, 2, 32], F32)
        G = self.av(72 * KB, [NJ, 2, 32], F32)
        Sall = self.av(76 * KB, [NJ, 2, 32], F32)
        self.proj_fm(l, 0, 1024, lambda j, o: self.cp('act', uT[:, j, :], o))
        self.proj_fm(l, 1024, 1024, lambda j, o: self.act(sz[:, j, :], o, AF.Silu))
        if sample:
            Zs = self.merged[:, :, :].rearrange("p a t -> p (a t)").bitcast(F32).rearrange(
                "p (s c q) -> p s c q", s=NSEQ, c=2, q=32)
        bcA = lambda tb, n: tb.unsqueeze(1).broadcast_to([128, n, 2, 32])
        A1, B1 = bcA(self.lamA[:, l, :, :], NJ), bcA(self.lamB[:, l, :, :], NJ)
        BUH4 = BUH.rearrange("p (n s) c q -> p n s c q", n=NJ, s=8)
        for sbk in range(2):
            tok0 = sbk * 128
            for pt0 in range(0, 32, 2):
                bk = self.s.bank()
                items = []
                for dp in range(2):
                    pt = pt0 + dp
                    ct, pm = pt // 4, pt % 4
                    hf, v = pm // 2, pm % 2
                    for c in range(2):
                        base = ((c * 8 + ct) * 2 + v) * 128
                        items.append((bk[:, (dp * 2 + c) * 128:(dp * 2 + c + 1) * 128],
                                      self.buw[hf * 64:(hf + 1) * 64, base:base + 128],
                                      uT[hf * 64:(hf + 1) * 64, ct, tok0:tok0 + 128]))
                self.mm_multi(items)
                dst = BUH[:, :, :, pt0:pt0 + 2].rearrange("p t c q -> p q c t")
                self.cp('act', dst, bk[:, :].rearrange("p (q c t) -> p q c t", q=2, c=2, t=128))
            if sample and sbk == 0:
                self.load_s5_state(l, Zs)
            for st in range(1, 8):
                prev = BUH4[:, :, 0, :, :] if st == 1 else G
                self.cstep(G, prev, BUH4[:, :, st, :, :], A1, B1, t1, t2)
                self.tick()
            if sample:
                Z0 = Zs[:, sbk * 16:(sbk + 1) * 16, :, :]
                self.cp('dve', Sall, Z0)
                self.cstep(Z0, Sall, G, bcA(self.lam8A[:, l, :, :], 16), bcA(self.lam8B[:, l, :, :], 16), t1, t2)
            else:
                Zst = self.Zp[:, l, :, :].unsqueeze(1)
                A8, B8 = bcA(self.lam8A[:, l, :, :], 1), bcA(self.lam8B[:, l, :, :], 1)
                self.cp('dve', Sall[:, 0:1, :, :], Zst)
                for j in range(16):
                    dstS = Sall[:, j + 1:j + 2, :, :] if j < 15 else Zst
                    self.cstep(dstS, Sall[:, j:j + 1, :, :], G[:, j:j + 1, :, :], A8, B8,
                               t1[:, 0:1, :, :], t2[:, 0:1, :, :])
            for st in range(8):
                prev = Sall if st == 0 else BUH4[:, :, st - 1, :, :]
                self.cstep(BUH4[:, :, st, :, :], prev, BUH4[:, :, st, :, :], A1, B1, t1, t2)
                self.tick()
            for ct in range(8):
                hb = hbs[ct % 2]
                self.cp('act', hb, BUH[:, :, :, 4 * ct:4 * ct + 4])
                bk = self.s.bank()
                for hf in range(2):
                    pairs = []
                    for pm2 in range(2):
                        pt = 4 * ct + 2 * hf + pm2
                        for c in range(2):
                            base = c * 32 * 64 + pt * 64
                            pairs.append((self.cw[:, base:base + 64], hb[:, :, c, pt % 4]))
                    self.mm(bk[hf * 64:(hf + 1) * 64, 0:128], pairs)
                self.stt('dve', yT[:, ct, tok0:tok0 + 128], uT[:, ct, tok0:tok0 + 128],
                         cv[:, O_D + ct:O_D + ct + 1], bk[:, 0:128], ALU.mult, ALU.add)
        if sample:
            self.store_s5_state(l, Zs)
        ygb = self.av(0, [8, T], BF16)
        sig = self.av(8 * KB, [T], F32)
        sig2 = self.av(9 * KB, [T], F32)
        gq = self.av(16 * KB, [8, T], F32)
        self.act(gq, yT, AF.Square)
        self.ts('dve', gq, gq, 0.044715, 1.0, ALU.mult, ALU.add)
        self.tt('dve', gq, gq, yT, ALU.mult)
        self.act(gq, gq, AF.Sigmoid, scale=1.5957691216057308)
        self.tt('dve', yT, yT, gq, ALU.mult)
        self.cp('dve', ygb, yT)
        for c0 in range(0, 1024, 512):
            w = self.load_w((l, 'glu', c0), self.w_glu[l][:, c0:c0 + 512], 1024, 512)

            def ev(j, o, c0=c0):
                ct = c0 // 128 + j
                sg_ = sig if ct % 2 == 0 else sig2
                self.act(sg_, o, AF.Sigmoid, bias=cv[:, O_BGLU + ct:O_BGLU + ct + 1])
                self.tt('dve', sg_, sg_, yT[:, ct, :], ALU.mult)
                self.tt('dve', ysb[:, ct, :], sg_, sz[:, ct, :], ALU.mult)
            self.fm_chunk(w, 8, 4, lambda k: ygb[:, k, :], ev)

    def load_s5_state(self, l, Zs):
        stg = self.av(64 * 1024, [2048], F32)
        for c, src in ((0, self.st_s5re), (1, self.st_s5im)):
            for hv in range(2):
                self.dma(stg[0:NSEQ, :], src[l][:, hv * 2048:(hv + 1) * 2048])
                bk = self.s.bank()
                self.tr_multi([(bk[:, q * NSEQ:(q + 1) * NSEQ], stg[0:NSEQ, q * 128:(q + 1) * 128]) for q in range(16)],
                              self.ident_f[0:NSEQ, 0:NSEQ])
                self.cp('act', Zs[:, :, c, hv * 16:(hv + 1) * 16].rearrange("p s q -> p q s"),
                        bk[:, :].rearrange("p (q s) -> p q s", q=16, s=NSEQ))

    def store_s5_state(self, l, Zs):
        stg = self.av(64 * 1024, [2048], F32)
        for c in range(2):
            for hv in range(2):
                for q4 in range(4):
                    bk = self.s.bank()
                    self.tr_multi([(bk[0:NSEQ, j * 128:(j + 1) * 128], Zs[:, :, c, hv * 16 + q4 * 4 + j]) for j in range(4)],
                                  self.ident_f)
                    self.cp('act', stg[0:NSEQ, q4 * 512:(q4 + 1) * 512], bk[0:NSEQ, :])
                self.dma(self.o_s5_s[c][l][:, hv * 2048:(hv + 1) * 2048], stg[0:NSEQ, :], is_output=True)

    def prompt_state_out(self):
        stg = self.av(0, [2, 2, 128], F32)
        for l in range(2):
            for c in range(2):
                self.tr32(stg[0:32, l, c, :], self.Zp[:, l, c, :], 128, 32, evac='act')
                self.dma(self.o_s5_p[c][l].rearrange("(a b) -> a b", b=128), stg[0:32, l, c, :], is_output=True)
        stg2 = self.av(8192, [2, 128], F32)
        for l in range(2):
            self.tr32(stg2[0:8, l, :], self.hst[:, l, :], 128, 8, evac='act')
            self.dma(self.o_lru_p[l].rearrange("(a b) -> a b", b=128), stg2[0:8, l, :], is_output=True)
        stg3 = self.av(12288, [2, 3, 128], F32)
        for l in range(2):
            for k in range(3):
                self.tr32(stg3[0:8, l, k, :], self.convc[:, l, :, k], 128, 8, evac='act')
                self.dma(self.o_conv_p[l, k].rearrange("(a b) -> a b", b=128), stg3[0:8, l, k, :], is_output=True)
        for l in range(2):
            self.dma(self.o_ret_p[l].rearrange("h d e -> d h e"), self.Rp[:, l, :, :], is_output=True)

    def lru(self, l, blk, sample, ysb):
        cv = self.cvec[:, l, :]
        KB = 1024
        nseq, L = (NSEQ, 8) if sample else (1, T)
        xp_ = self.av(24 * KB, [8, nseq, 3 + L], F32)
        xc = self.av(64 * KB, [8, nseq, L], F32)
        xcb = self.av(76 * KB, [8, T], BF16)
        r_all = self.av(24 * KB, [8, nseq, L], F32)
        szl = self.av(36 * KB, [8, T], BF16)
        i_all = self.av(56 * KB, [8, nseq, L], F32)
        a_all = self.av(8 * KB, [8, nseq, L], F32)
        m_all = self.av(80 * KB, [8, nseq, L], F32)
        hl = self.av(20 * KB, [8, NSEQ], F32)
        t8 = self.av(21 * KB, [8, NSEQ], F32)
        stg = self.av(16 * KB, [1024], F32)
        v3 = lambda o: o.rearrange("p (s t) -> p s t", s=nseq, t=L)
        self.proj_fm(l, 2048, 1024, lambda j, o: self.cp('act', xp_[:, j, :, 3:3 + L], v3(o)))
        if self.sg is None:
            self.proj_fm(l, 3072, 1024, lambda j, o: self.act(szl[:, j, :], o, AF.Silu))
        else:
            self.drain(56)
        if sample:
            for k in range(3):
                self.dma(stg[0:NSEQ, :], self.st_conv[l, :, k, :])
                for ct in range(8):
                    self.tr32(xp_[:, ct, :, k], stg[0:NSEQ, ct * 128:(ct + 1) * 128], NSEQ, 128, evac='act')
            self.dma(stg[0:NSEQ, :], self.st_lru[l])
            for ct in range(8):
                self.tr32(hl[:, ct, :], stg[0:NSEQ, ct * 128:(ct + 1) * 128], NSEQ, 128, evac='act')
        else:
            self.cp('dve', xp_[:, :, 0, 0:3], self.convc[:, l, :, :])
        for ct in range(8):
            cw = lambda k: cv[:, O_CW + k * 8 + ct: O_CW + k * 8 + ct + 1]
            self.act(xc[:, ct, :, :], xp_[:, ct, :, 0:L], AF.Identity, bias=cv[:, O_CB + ct:O_CB + ct + 1], scale=cw(0))
            for k in range(1, 4):
                self.stt('dve', xc[:, ct, :, :], xp_[:, ct, :, k:k + L], cw(k), xc[:, ct, :, :], ALU.mult, ALU.add)
            self.cp('act', xcb[:, ct, :], xc[:, ct, :, :].rearrange("p s t -> p (s t)"))
        if sample:
            for k in range(3):
                for ct in range(8):
                    self.tr32(stg[0:NSEQ, ct * 128:(ct + 1) * 128], xp_[:, ct, :, L + k], 128, NSEQ, evac='act')
                self.dma(self.o_conv_s[l, :, k, :], stg[0:NSEQ, :], is_output=True)
        else:
            self.cp('dve', self.convc[:, l, :, :], xp_[:, :, 0, L:L + 3])
        yield
        wa = self.load_w((l, 'wa'), self.w_a[l], 1024, 128)
        wx = self.load_w((l, 'wx'), self.w_x[l], 1024, 128)
        f3 = lambda v: v.rearrange("p a s t -> p a (s t)")
        f2 = lambda v: v.rearrange("p a s t -> p (a s t)")
        for n in range(8):
            bk = self.s.bank()
            self.mm(bk[:, 0:T], [(wa[:, n, :], xcb[:, n, :])])
            self.act(f3(r_all)[:, n, :], bk[:, 0:T], AF.Sigmoid, bias=cv[:, O_BA + n:O_BA + n + 1])
            bk2 = self.s.bank()
            self.mm(bk2[:, 0:T], [(wx[:, n, :], xcb[:, n, :])])
            self.act(f3(i_all)[:, n, :], bk2[:, 0:T], AF.Sigmoid, bias=cv[:, O_BX + n:O_BX + n + 1])
        self.tt('dve', f3(a_all), f3(r_all), self.c1[:, l, :].unsqueeze(2).broadcast_to([128, 8, T]), ALU.mult)
        self.act(f2(a_all), f2(a_all), AF.Exp)
        self.act(f2(m_all), f2(a_all), AF.Square)
        self.act(f2(m_all), f2(m_all), AF.Sqrt, bias=1.0, scale=-1.0)
        self.tt('dve', f2(m_all), f2(m_all), f2(i_all), ALU.mult)
        self.tt('dve', f2(m_all), f2(m_all), f2(xc), ALU.mult)
        if sample:
            self.tt('dve', t8, a_all[:, :, :, 0], hl, ALU.mult)
            self.tt('dve', m_all[:, :, :, 0], m_all[:, :, :, 0], t8, ALU.add)
        else:
            self.tt('dve', t8[:, :, 0:1], a_all[:, :, :, 0], self.hst[:, l, :].unsqueeze(2), ALU.mult)
            self.tt('dve', m_all[:, :, :, 0], m_all[:, :, :, 0], t8[:, :, 0:1], ALU.add)
        self.memset('dve', a_all[:, :, :, 0], 0.0)
        hT = r_all
        self.scan(f2(hT), f2(a_all), f2(m_all))
        if sample:
            self.cp('dve', hl, hT[:, :, :, L - 1])
        else:
            self.cp('dve', self.hst[:, l, :].unsqueeze(2), hT[:, :, :, L - 1])
        self.tt('dve', ysb.rearrange("p a t -> p (a t)"), f2(hT), szl.rearrange("p a t -> p (a t)"), ALU.mult)
        if sample:
            for ct in range(8):
                self.tr32(stg[0:NSEQ, ct * 128:(ct + 1) * 128], hl[:, ct, :], 128, NSEQ, evac='act')
            self.dma(self.o_lru_s[l], stg[0:NSEQ, :], is_output=True)

    def ret(self, l, blk, sample, ysb):
        cv = self.cvec[:, l, :]
        KB = 1024
        grp = 1 if sample else 0
        rt = self.av(0, [4, T], F32)
        rset = [[self.av((4 + j) * KB, [T], F32) for j in range(3)], [self.av((32 + j) * KB, [T], F32) for j in range(3)]]
        qr = self.av(8 * KB, [8, T], BF16); kr = self.av(12 * KB, [8, T], BF16); qxi = self.av(16 * KB, [8, T], BF16)
        oT = self.av(20 * KB, [8, T], F32)
        kzs = [self.av(28 * KB + 256 * j, [128], BF16) for j in range(2)]
        Sms = [self.av(28 * KB + 512 + 256 * j, [128], BF16) for j in range(2)]
        gset = [[self.av((29 + j) * KB, [T], F32) for j in range(3)], [self.av((56 + j) * KB, [T], F32) for j in range(3)]]
        vtok = self.av(40 * KB, [2, 1024], BF16); szr = self.av(44 * KB, [8, T], BF16)
        msk = self.av(48 * KB, [2, 8, 128], F32)
        R0f = self.av(60 * KB, [16, 128], F32); R0b = self.av(68 * KB, [16, 128], BF16)
        Vblk = self.av(76 * KB, [16, 128], BF16); Rn = self.av(80 * KB, [16, 128], F32)
        self.dma(rt, self.rope_s if sample else self.rope_p[:, :, blk * T:(blk + 1) * T])
        self.dma(msk, self.rmask_d[grp])
        zeta = self.rzeta[:, grp, :]
        xiv = lambda h: msk[:, 1, h, :].unsqueeze(1).broadcast_to([128, 2, 128])

        def rope(o, h, dst, ci, si, with_xi):
            qf, t1, t3 = rset[h % 2]
            self.cp('act', qf, o)
            bk = self.s.bank()
            self.mm(bk[:, 0:T], [(self.swap_f, qf)])
            self.tt('dve', t1, qf, rt[:, ci, :], ALU.mult)
            self.tt('dve', t3, bk[:, 0:T], rt[:, si, :], ALU.mult)
            self.tt('dve', t3, t3, t1, ALU.add)
            self.cp('dve', dst[:, h, :], t3)
            if with_xi:
                self.tt('dve', qxi[:, h, :].rearrange("p (a b) -> p a b", a=2, b=128),
                        t3.rearrange("p (a b) -> p a b", a=2, b=128), xiv(h), ALU.mult)
        self.proj_fm(l, 4096, 1024, lambda j, o: rope(o, j, qr, 0, 1, True))
        self.proj_fm(l, 5120, 1024, lambda j, o: rope(o, j, kr, 2, 3, False))
        if self.sg is None:
            for _ in self.gen_v(l, vtok):
                pass
            self.proj_fm(l, 7168, 1024, lambda j, o: self.act(szr[:, j, :], o, AF.Silu))
        else:
            self.drain(72)
        yield
        for ck in range(2):
            for h in range(8):
                kz, Sm = kzs[h % 2], Sms[h % 2]
                tk = slice(ck * 128, (ck + 1) * 128)
                vh = vtok[:, ck, h * 128:(h + 1) * 128]
                bk = self.s.bank()
                kb = bk[:, 0:64].bitcast(BF16)
                self.tr_multi([(kb, kr[:, h, tk])], self.ident_b[:, :])
                self.act(kz, kb, AF.Identity, scale=zeta[:, h:h + 1])
                bk1 = self.s.bank()
                self.mm(bk1[:, 0:128], [(kr[:, h, tk], qr[:, h, tk])])
                self.tt('dve', Sm, bk1[:, 0:128], msk[:, 0, h, :], ALU.mult)
                if not sample:
                    bk2 = self.s.bank()
                    self.mm(bk2[:, 0:128], [(vh, Sm), (self.Rpb[:, l, h, :], qxi[:, h, tk])])
                    self.cp('act', oT[:, h, tk], bk2[:, 0:128])
                    bk3 = self.s.bank()
                    self.mm(bk3[:, 0:128], [(kz, vh)])
                    self.stt('dve', self.Rp[:, l, h, :], self.Rp[:, l, h, :], RET_G[h] ** 128, bk3[:, 0:128],
                             ALU.mult, ALU.add)
                    self.cp('dve', self.Rpb[:, l, h, :], self.Rp[:, l, h, :])
                else:
                    sq0 = ck * 16
                    self.dma(R0f, self.st_ret[l, sq0:sq0 + 16, h].rearrange("s d e -> d s e"))
                    self.cp('dve', R0b, R0f)
                    bk2 = self.s.bank()
                    self.mm(bk2[:, 0:128], [(vh, Sm)])
                    self.cp('act', oT[:, h, tk], bk2[:, 0:128])
                    bk4 = self.s.bank()
                    self.mm_multi([(bk4[:, s * 8:(s + 1) * 8], R0b[:, s, :], qxi[:, h, ck * 128 + s * 8: ck * 128 + (s + 1) * 8])
                                   for s in range(16)])
                    self.tt('dve', oT[:, h, tk], oT[:, h, tk], bk4[:, 0:128], ALU.add)
                    self.tt('dve', Vblk, vh.unsqueeze(1).broadcast_to([128, 16, 128]),
                            self.bm[:, :].unsqueeze(2).broadcast_to([128, 16, 128]), ALU.mult)
                    for q4 in range(4):
                        bk3 = self.s.bank()
                        self.mm(bk3[:, :], [(kz, Vblk[:, q4 * 4:(q4 + 1) * 4, :].rearrange("p a b -> p (a b)"))])
                        self.stt('dve', Rn[:, q4 * 4:(q4 + 1) * 4, :], R0f[:, q4 * 4:(q4 + 1) * 4, :], RET_G[h] ** 8,
                                 bk3[:, :].rearrange("p (a b) -> p a b", a=4, b=128), ALU.mult, ALU.add)
                    self.dma(self.o_ret_s[l, sq0:sq0 + 16, h].rearrange("s d e -> d s e"), Rn, is_output=True)
        osq = self.av(0, [8, T], F32)
        self.act(osq.rearrange("p a t -> p (a t)"), oT.rearrange("p a t -> p (a t)"), AF.Square)
        for h in range(8):
            g1, g2, g3 = gset[h % 2]
            o_h = oT[:, h, :]
            bkm = self.s.bank()
            self.mm(bkm[:, 0:T], [(self.ones_f, o_h)])
            bkv = self.s.bank()
            self.mm(bkv[:, 0:T], [(self.ones_f, osq[:, h, :])])
            self.act(g2, bkm[:, 0:T], AF.Identity, scale=1.0 / 128)
            self.tt('dve', g3, g2, g2, ALU.mult)
            self.stt('dve', g3, bkv[:, 0:T], 1.0 / 128, g3, ALU.mult, ALU.subtract)
            self.act(g3, g3, AF.Sqrt, bias=1e-5)
            self.recip(g3, g3)
            self.tt('dve', g1, o_h, g2, ALU.subtract)
            self.tt('dve', g1, g1, g3, ALU.mult)
            self.stt('dve', ysb[:, h, :], g1, cv[:, O_GN + h:O_GN + h + 1], szr[:, h, :], ALU.mult, ALU.mult)


_NC = None


def _get_nc():
    global _NC
    if _NC is None:
        _NC = B().nc
    return _NC


def _vec(v, n):
    return np.ascontiguousarray(v.reshape(n, 128).T)


def _consts():
    c = {}
    ident = np.eye(128, dtype=np.float32)
    ones = np.ones((128, 128), np.float32)
    swap = np.zeros((128, 128), np.float32)
    for m in range(128):
        swap[(m + 64) % 128, m] = 1.0
    c['cmat'] = np.ascontiguousarray(np.stack([ident, ones, swap], axis=1))

    def rope(pos):
        half = 64
        freq = (np.float32(10000.0) ** (-np.arange(half, dtype=np.float32) / np.float32(half))).astype(np.float32)
        ang = (pos.astype(np.float32)[:, None] * freq[None, :]).astype(np.float32)
        cos = np.cos(ang).astype(np.float32).T; sin = np.sin(ang).astype(np.float32).T
        cos2 = np.concatenate([cos, cos], 0); sinS = np.concatenate([-sin, sin], 0)
        sc = np.float32(128.0 ** -0.5)
        return np.ascontiguousarray(np.stack([cos2, sinS, cos2 * sc, sinS * sc], axis=1).astype(np.float32))
    c['rope_p'] = rope(np.arange(2048, dtype=np.float32))
    c['rope_s'] = rope(np.tile(np.arange(8, dtype=np.float32) + 16384.0, 32))
    log_g = np.log1p(-np.exp2(-5.0 - np.arange(8, dtype=np.float64)))
    rmask = np.zeros((2, 128, 2, 8, 128), np.float32)
    rzeta = np.zeros((2, 128, 8), np.float32)
    j = np.arange(128)[:, None]; i = np.arange(128)[None, :]
    for h in range(8):
        dm = np.where(i >= j, np.exp((i - j).clip(0) * log_g[h]), 0.0)
        rmask[0, :, 0, h, :] = dm
        rmask[0, :, 1, h, :] = np.exp((i + 1) * log_g[h])
        rzeta[0, :, h] = np.exp((127 - np.arange(128)) * log_g[h])
        same = (i // 8) == (j // 8)
        dms = np.where(same & (i >= j), np.exp((i - j).clip(0) * log_g[h]), 0.0)
        rmask[1, :, 0, h, :] = dms
        rmask[1, :, 1, h, :] = np.exp(((i % 8) + 1) * log_g[h])
        rzeta[1, :, h] = np.exp((7 - np.arange(128) % 8) * log_g[h])
    c['rmask'] = rmask; c['rzeta'] = rzeta
    bm = np.zeros((128, 16), np.float32)
    bm[np.arange(128), np.arange(128) // 8] = 1.0
    c['bm'] = bm
    return c


def kernel(**inp):
    f = lambda k: np.asarray(inp[k], dtype=np.float32)
    shared = _consts()
    shared['w_in'] = f('w_in'); shared['w_glu'] = f('s5_w_glu')
    shared['w_a'] = np.ascontiguousarray(f('lru_w_a').reshape(2, 1024, 128))
    shared['w_x'] = np.ascontiguousarray(f('lru_w_x').reshape(2, 1024, 128))
    shared['wb_s5'] = f('w_branch_s5'); shared['wb_lru'] = f('w_branch_lru'); shared['wb_ret'] = f('w_branch_ret')
    shared['w_out'] = f('w_out')
    cvec = np.zeros((2, 128, NCV), np.float32)
    s5tab = np.zeros((2, 128, S5TW), np.float32)
    for l in range(2):
        cvec[l, :, O_NG:O_NG + 16] = _vec(f('norm_g')[l], 16)
        cvec[l, :, O_D:O_D + 8] = _vec(f('s5_d')[l], 8)
        cvec[l, :, O_BGLU:O_BGLU + 8] = _vec(f('s5_b_glu')[l], 8)
        for k in range(4):
            cvec[l, :, O_CW + 8 * k:O_CW + 8 * k + 8] = _vec(f('lru_conv_w')[l, k], 8)
        cvec[l, :, O_CB:O_CB + 8] = _vec(f('lru_conv_b')[l], 8)
        cvec[l, :, O_BA:O_BA + 8] = _vec(f('lru_b_a')[l], 8)
        cvec[l, :, O_BX:O_BX + 8] = _vec(f('lru_b_x')[l], 8)
        cvec[l, :, O_LAM:O_LAM + 8] = _vec(f('lru_lambda')[l], 8)
        cvec[l, :, O_GN:O_GN + 8] = _vec(f('ret_gn_g')[l], 8)
        cvec[l, :, O_FNG:O_FNG + 16] = _vec(f('final_norm_g'), 16)
        pl = lambda a: np.ascontiguousarray(a.reshape(32, 2, 64).transpose(1, 2, 0).reshape(128, 32))
        s5tab[l, :, 0:32] = pl(f('s5_lambda_re')[l])
        s5tab[l, :, 32:64] = pl(f('s5_lambda_im')[l])
        s5tab[l, :, 64:96] = pl(np.repeat(f('s5_log_dt')[l][:, None], 64, axis=1))
        pl3 = lambda a: np.ascontiguousarray(a.reshape(32, 2, 64, 16).transpose(1, 2, 0, 3).reshape(128, 512))
        s5tab[l, :, 96:608] = pl3(f('s5_b_re')[l])
        s5tab[l, :, 608:1120] = pl3(f('s5_b_im')[l])
        s5tab[l, :, 1120:1632] = pl3(f('s5_c_re')[l].transpose(0, 2, 1))
        s5tab[l, :, 1632:2144] = pl3(f('s5_c_im')[l].transpose(0, 2, 1))
    shared['cvec'] = cvec; shared['s5tab'] = s5tab
    in_maps = []
    for c in range(8):
        b = c % 4
        m = dict(shared)
        m['xp'] = np.ascontiguousarray(f('x_prompt')[b])
        sl = slice(32 * b, 32 * b + 32)
        m['xs'] = np.ascontiguousarray(f('x_sample')[sl].reshape(256, D))
        m['st_s5re'] = np.ascontiguousarray(f('state_s5_re')[:, sl].reshape(2, 32, 4096))
        m['st_s5im'] = np.ascontiguousarray(f('state_s5_im')[:, sl].reshape(2, 32, 4096))
        m['st_lru'] = np.ascontiguousarray(f('state_lru')[:, sl])
        m['st_conv'] = np.ascontiguousarray(f('state_conv')[:, sl])
        m['st_ret'] = np.ascontiguousarray(f('state_ret')[:, sl])
        in_maps.append(m)
    nc = _get_nc()
    res = run_bass_kernel_spmd(nc, in_maps, core_ids=list(range(8)))
    R = res.results[:4]
    cat = lambda k, ax: np.concatenate([r[k] for r in R], axis=ax)
    y_p = np.stack([r['yp'] for r in R], 0)
    y_s = cat('ys', 0).reshape(128, 8, D)
    pst = lambda k, shp: np.stack([r[k] for r in R], 1).reshape(shp)
    outs = (y_p, y_s,
            pst('o_s5re_p', (2, 4, 64, 64)), pst('o_s5im_p', (2, 4, 64, 64)),
            pst('o_lru_p', (2, 4, 1024)), pst('o_conv_p', (2, 4, 3, 1024)), pst('o_ret_p', (2, 4, 8, 128, 128)),
            cat('o_s5re_s', 1).reshape(2, 128, 64, 64), cat('o_s5im_s', 1).reshape(2, 128, 64, 64),
            cat('o_lru_s', 1), cat('o_conv_s', 1), cat('o_ret_s', 1))
    return tuple(np.ascontiguousarray(o, dtype=np.float32) for o in outs)
```

```python
import numpy as np
import concourse.bass as bass
import concourse.mybir as mybir
from concourse.bass_utils import run_bass_kernel_spmd

F32 = mybir.dt.float32
BF16 = mybir.dt.bfloat16
AF = mybir.ActivationFunctionType
ALU = mybir.AluOpType
ESZ = {F32: 4, BF16: 2}

D = 2048; T = 256; NBLK_P = 8; NSEQ = 32; DEPTH = 2; NIN = 14336
EPOCH = 4096; NDS = 32
NCV = 120
O_NG, O_D, O_BGLU, O_CW, O_CB, O_BA, O_BX, O_LAM, O_GN, O_FNG = 0, 16, 24, 32, 64, 72, 80, 88, 96, 104
S5TW = 96 + 4 * 512
RET_G = [float(1.0 - 2.0 ** (-5.0 - h)) for h in range(8)]


class Sch:
    def __init__(self, nc):
        self.nc = nc
        self.eng = {'pe': nc.tensor, 'act': nc.scalar, 'dve': nc.vector, 'pool': nc.gpsimd, 'sp': nc.sync}
        self.cnt = {e: 0 for e in self.eng}
        self.sems = {e: [] for e in self.eng}
        self.seen = {e: {} for e in self.eng}
        self.recs = {}
        self.dsem = [nc.alloc_semaphore(name=f"dq{i}") for i in range(NDS)]
        self.dval = [0] * NDS
        self.dnext = 0
        self.dnext_sw = 0
        self.out_dmas = []
        self.nbank = 0
        self.banks = [nc.alloc_psum_tensor(f"bank{i}", [128, 512], F32) for i in range(8)]

    def bank(self):
        b = self.banks[self.nbank % 8]
        self.nbank += 1
        return b

    def esem(self, e, ep):
        while len(self.sems[e]) <= ep:
            self.sems[e].append(self.nc.alloc_semaphore(name=f"e_{e}_{len(self.sems[e])}"))
        return self.sems[e][ep]

    def iv(self, a):
        sp = str(a.space)
        name = a.tensor.name
        if 'DRAM' in sp.upper():
            if name.startswith('scr'):
                return (name, 0, 1 << 40)
            return None
        ap = a.ap
        pstep = ap[0][0]
        off = a.offset % pstep if pstep > 0 else a.offset
        lo = off; hi = off + 1
        for st, cn in ap[1:]:
            if st >= 0:
                hi += (cn - 1) * st
            else:
                lo += (cn - 1) * st
        es = ESZ[a.dtype]
        return (name, lo * es, hi * es)

    def _wait(self, e, tgt):
        if tgt[0] == 'E':
            _, te, n = tgt
            if self.seen[e].get(te, 0) >= n:
                return
            self.seen[e][te] = n
            self.eng[e].wait_ge(self.esem(te, (n - 1) // EPOCH), (n - 1) % EPOCH + 1)
        else:
            _, k, v = tgt
            if self.seen[e].get(('D', k), 0) >= v:
                return
            self.seen[e][('D', k)] = v
            self.eng[e].wait_ge(self.dsem[k], v)

    def _sync(self, e, R, W):
        for (name, lo, hi) in R:
            for r in self.recs.get(name, ()):
                if r[2] and r[0] < hi and lo < r[1]:
                    t = r[3]
                    if t[0] == 'E' and t[1] == e and e in ('pe',):
                        continue
                    self._wait(e, t)
        for (name, lo, hi) in W:
            for r in self.recs.get(name, ()):
                if r[0] < hi and lo < r[1]:
                    t = r[3]
                    if t[0] == 'E' and t[1] == e and e == 'pe':
                        continue
                    self._wait(e, t)

    def _record(self, R, W, tgt):
        for (name, lo, hi) in W:
            L = self.recs.setdefault(name, [])
            L[:] = [r for r in L if not (lo <= r[0] and r[1] <= hi)]
            L.append((lo, hi, True, tgt))
        for (name, lo, hi) in R:
            L = self.recs.setdefault(name, [])
            if tgt[0] == 'E':
                L[:] = [r for r in L if not ((not r[2]) and r[3][0] == 'E' and r[3][1] == tgt[1]
                                             and lo <= r[0] and r[1] <= hi)]
            L.append((lo, hi, False, tgt))

    def op(self, e, fn, reads=(), writes=()):
        R = [x for x in (self.iv(a) for a in reads) if x is not None]
        W = [x for x in (self.iv(a) for a in writes) if x is not None]
        self._sync(e, R, W)
        ins = fn()
        self.cnt[e] += 1
        n = self.cnt[e]
        ins.then_inc(self.esem(e, (n - 1) // EPOCH), 1)
        self._record(R, W, ('E', e, n))

    def dma(self, q, out, in_, is_output=False):
        R = [x for x in (self.iv(in_),) if x is not None]
        W = [x for x in (self.iv(out),) if x is not None]
        self._sync(q, R, W)
        if q == 'pool':
            k = NDS - 8 + (self.dnext_sw % 8); self.dnext_sw += 1
        else:
            k = self.dnext % (NDS - 8)
        self.dnext += 1
        if self.dval[k] > 0:
            self._wait(q, ('D', k, self.dval[k]))
        self.dval[k] += 16
        self.eng[q].dma_start(out=out, in_=in_).then_inc(self.dsem[k], 16)
        tgt = ('D', k, self.dval[k])
        self._record(R, W, tgt)
        if is_output:
            self.out_dmas.append(tgt)

    def finish(self):
        for k in range(NDS):
            if self.dval[k] > 0:
                self._wait('sp', ('D', k, self.dval[k]))
        for e in ('pe', 'act', 'dve', 'pool'):
            if self.cnt[e] > 0:
                self._wait('sp', ('E', e, self.cnt[e]))


def aps(*xs):
    return [x for x in xs if hasattr(x, 'ap') and hasattr(x, 'tensor')]


class B:
    def __init__(self):
        nc = bass.Bass("TRN2", target_bir_lowering=False)
        self.nc = nc
        self.s = Sch(nc)
        self.build()

    def mm(self, out, pairs):
        nc = self.nc
        reads = []
        for l, r in pairs:
            reads += [l, r]

        def fn():
            ins = None
            n = len(pairs)
            for i, (l, r) in enumerate(pairs):
                ins = nc.tensor.matmul(out, l, r, start=(i == 0), stop=(i == n - 1))
            return ins
        self.s.op('pe', fn, reads, [out])

    def mm_multi(self, items):
        nc = self.nc
        reads = []; writes = []
        for o, l, r in items:
            reads += [l, r]; writes.append(o)

        def fn():
            ins = None
            for o, l, r in items:
                ins = nc.tensor.matmul(o, l, r, start=True, stop=True)
            return ins
        self.s.op('pe', fn, reads, writes)

    def tr_multi(self, items, ident):
        nc = self.nc
        reads = [ident]; writes = []
        for o, i in items:
            reads.append(i); writes.append(o)

        def fn():
            ins = None
            for o, i in items:
                ins = nc.tensor.transpose(o, i, ident)
            return ins
        self.s.op('pe', fn, reads, writes)

    def act(self, out, in_, func, bias=0.0, scale=1.0):
        nc = self.nc
        self.s.op('act', lambda: nc.scalar.activation(out, in_, func, bias=bias, scale=scale),
                  aps(in_, bias, scale), [out])

    def E(self, e):
        return {'dve': self.nc.vector, 'pool': self.nc.gpsimd}[e]

    def tt(self, e, out, in0, in1, op):
        self.s.op(e, lambda: self.E(e).tensor_tensor(out, in0, in1, op), [in0, in1], [out])

    def ts(self, e, out, in0, s1, s2, op0, op1=None):
        if op1 is None:
            self.s.op(e, lambda: self.E(e).tensor_scalar(out, in0, s1, None, op0), aps(in0, s1), [out])
        else:
            self.s.op(e, lambda: self.E(e).tensor_scalar(out, in0, s1, s2, op0, op1), aps(in0, s1, s2), [out])

    def stt(self, e, out, in0, scalar, in1, op0, op1):
        self.s.op(e, lambda: self.E(e).scalar_tensor_tensor(out, in0, scalar, in1, op0, op1),
                  aps(in0, scalar, in1), [out])

    def cp(self, e, out, in_):
        if e == 'act':
            self.act(out, in_, AF.Copy)
        else:
            self.s.op(e, lambda: self.E(e).tensor_copy(out, in_), [in_], [out])

    def memset(self, e, out, val):
        self.s.op(e, lambda: self.E(e).memset(out, val), [], [out])

    def recip(self, out, in_):
        self.s.op('dve', lambda: self.nc.vector.reciprocal(out, in_), [in_], [out])

    def scan(self, out, d0, d1):
        self.s.op('dve', lambda: self.nc.vector.tensor_tensor_scan(out, d0, d1, 0.0, ALU.mult, ALU.add),
                  [d0, d1], [out])

    def dma(self, out, in_, q='sp', is_output=False):
        self.s.dma(q, out, in_, is_output)

    def av(self, off, shape, dt):
        n = int(np.prod(shape))
        nb = n * ESZ[dt]
        assert off % 4 == 0 and off + nb <= self.AB, (off, nb, self.AB)
        v = self.arena[:, off // 4: off // 4 + (nb + 3) // 4]
        if dt != F32:
            v = v.bitcast(dt)
        if len(shape) == 1:
            return v
        names = "abcd"[:len(shape)]
        kw = {names[i]: shape[i] for i in range(len(shape))}
        return v.rearrange("p (" + " ".join(names) + ") -> p " + " ".join(names), **kw)

    def tr32(self, dst, src, K, M, evac='act'):
        bk = self.s.bank()
        o = bk[0:M, 0:K]
        self.tr_multi([(o, src)], self.ident_f[0:K, 0:K])
        self.cp(evac, dst, o)

    def load_w(self, key, dram2d, rows, cols):
        slot = self.wslot % 4
        self.wslot += 1
        kt = rows // 128
        assert kt * cols <= 4096
        flat = self.wbuf[slot][:, 0: kt * cols]
        view = flat.rearrange("p (k c) -> p k c", k=kt, c=cols)
        if key not in self.scr:
            scr = self.nc.dram_tensor(f"scr_{len(self.scr)}", [128, kt * cols], BF16, kind="Internal").ap()
            self.scr[key] = scr
            src = dram2d.rearrange("(k p) c -> p k c", p=128)
            step = 8
            for k0 in range(0, kt, step):
                k1 = min(kt, k0 + step)
                self.dma(view[:, k0:k1, :], src[:, k0:k1, :], q='pool')
            self.dma(scr, flat, q='sp')
        else:
            self.dma(flat, self.scr[key], q='sp')
        return view

    def fm_chunk(self, w, kt, ncol_tiles, rhs_fn, evac):
        for m in range(ncol_tiles):
            bk = self.s.bank()
            o = bk[:, 0:T]
            self.mm(o, [(w[:, k, m * 128:(m + 1) * 128], rhs_fn(k)) for k in range(kt)])
            evac(m, o)

    def proj_fm(self, l, col0, ncols, evac):
        for c0 in range(0, ncols, 256):
            w = self.load_w((l, 'in', col0 + c0), self.w_in[l][:, col0 + c0: col0 + c0 + 256], 2048, 256)
            self.fm_chunk(w, 16, 2, lambda k: self.xnT[:, k, :], lambda m, o, c0=c0: evac(c0 // 128 + m, o))

    def build(self):
        nc = self.nc
        di = lambda name, shape: nc.dram_tensor(name, shape, F32, kind="ExternalInput").ap()
        do = lambda name, shape: nc.dram_tensor(name, shape, F32, kind="ExternalOutput").ap()
        self.xp = di("xp", [2048, D]); self.xs = di("xs", [256, D])
        self.st_s5re = di("st_s5re", [2, NSEQ, 4096]); self.st_s5im = di("st_s5im", [2, NSEQ, 4096])
        self.st_lru = di("st_lru", [2, NSEQ, 1024]); self.st_conv = di("st_conv", [2, NSEQ, 3, 1024])
        self.st_ret = di("st_ret", [2, NSEQ, 8, 128, 128])
        self.w_in = di("w_in", [2, D, NIN]); self.w_glu = di("w_glu", [2, 1024, 1024])
        self.w_a = di("w_a", [2, 1024, 128]); self.w_x = di("w_x", [2, 1024, 128])
        self.wb = [di("wb_s5", [2, 1024, D]), di("wb_lru", [2, 1024, D]), di("wb_ret", [2, 1024, D])]
        self.w_out = di("w_out", [2, D, D])
        self.cvec_d = di("cvec", [2, 128, NCV]); self.s5tab_d = di("s5tab", [2, 128, S5TW])
        self.cmat_d = di("cmat", [128, 3, 128])
        self.rope_p = di("rope_p", [128, 4, 2048]); self.rope_s = di("rope_s", [128, 4, 256])
        self.rmask_d = di("rmask", [2, 128, 2, 8, 128])
        self.rzeta_d = di("rzeta", [2, 128, 8]); self.bm_d = di("bm", [128, 16])
        self.yp = do("yp", [2048, D]); self.ys = do("ys", [256, D])
        self.o_s5_p = [do("o_s5re_p", [2, 4096]), do("o_s5im_p", [2, 4096])]
        self.o_lru_p = do("o_lru_p", [2, 1024]); self.o_conv_p = do("o_conv_p", [2, 3, 1024])
        self.o_ret_p = do("o_ret_p", [2, 8, 128, 128])
        self.o_s5_s = [do("o_s5re_s", [2, NSEQ, 4096]), do("o_s5im_s", [2, NSEQ, 4096])]
        self.o_lru_s = do("o_lru_s", [2, NSEQ, 1024]); self.o_conv_s = do("o_conv_s", [2, NSEQ, 3, 1024])
        self.o_ret_s = do("o_ret_s", [2, NSEQ, 8, 128, 128])

        sb = lambda name, shape, dt=F32: nc.alloc_sbuf_tensor("s_" + name, shape, dt)
        self.xT = sb("xT", [128, 16, T]); self.xnT = sb("xnT", [128, 16, T], BF16)
        self.merged = sb("merged", [128, 16, T], BF16)
        self.wbuf = [sb(f"wbuf{i}", [128, 4096], BF16) for i in range(4)]
        self.wslot = 0
        self.scr = {}
        self.cmat = sb("cmat", [128, 3, 128]); self.ident_f = self.cmat[:, 0, :]
        self.ones_f = self.cmat[:, 1, :]; self.swap_f = self.cmat[:, 2, :]
        self.ident_b = sb("ident_b", [128, 128], BF16); self.ones_b = sb("ones_b", [128, 128], BF16)
        self.cvec = sb("cvecs", [128, 2, NCV]); self.c1 = sb("c1", [128, 2, 8])
        self.rzeta = sb("rzeta", [128, 2, 8]); self.bm = sb("bm", [128, 16])
        self.buw = sb("buw", [128, 2 * 8 * 2 * 128], BF16)
        self.cw = sb("cw", [128, 2 * 32 * 64], BF16)
        self.s5w_scr = [nc.dram_tensor(f"scr_s5w{i}", [128, 8192], BF16, kind="Internal").ap() for i in range(2)]
        self.lamA = sb("lamA", [128, 2, 2, 32]); self.lamB = sb("lamB", [128, 2, 2, 32])
        self.lam8A = sb("lam8A", [128, 2, 2, 32]); self.lam8B = sb("lam8B", [128, 2, 2, 32])
        self.Zp = sb("Zp", [128, 2, 2, 32]); self.hst = sb("hst", [128, 2, 8]); self.convc = sb("convc", [128, 2, 8, 3])
        self.Rp = sb("Rp", [128, 2, 8, 128]); self.Rpb = sb("Rpb", [128, 2, 8, 128], BF16)
        self.AB = 104 * 1024
        self.arena = sb("arena", [128, self.AB // 4])

        self.dma(self.cmat[:, :, :], self.cmat_d)
        for l in range(2):
            self.dma(self.cvec[:, l, :], self.cvec_d[l])
            self.dma(self.rzeta[:, l, :], self.rzeta_d[l])
        self.dma(self.bm[:, :], self.bm_d)
        self.cp('dve', self.ident_b[:, :], self.ident_f)
        self.cp('dve', self.ones_b[:, :], self.ones_f)
        for t_ in (self.Zp, self.hst, self.convc, self.Rp, self.Rpb):
            shp = t_.shape
            v = t_[:, :, :, :] if len(shp) == 4 else t_[:, :, :]
            self.memset('dve', v, 0.0)
        for l in range(2):
            self.precompute(l)

        for blk in range(NBLK_P + 1):
            sample = (blk == NBLK_P)
            self.load_x(blk, sample)
            for l in range(2):
                self.layer(l, blk, sample)
            self.final_out(blk, sample)
            if blk == NBLK_P - 1:
                self.prompt_state_out()
        self.s.finish()

    def precompute(self, l):
        A = self
        cv = self.cvec[:, l, :]
        t0 = self.av(0, [8], F32)
        self.act(t0, cv[:, O_LAM:O_LAM + 8], AF.Exp, scale=-1.0)
        self.act(t0, t0, AF.Ln, bias=1.0)
        self.ts('dve', self.c1[:, l, :], t0, -8.0, None, ALU.mult)
        tab = self.av(1024, [S5TW], F32)
        self.dma(tab, self.s5tab_d[l])
        lamre = tab[:, 0:32]; lamim = tab[:, 32:64]; logdt = tab[:, 64:96]
        tb = lambda i: tab[:, 96 + i * 512: 96 + (i + 1) * 512].rearrange("p (a b) -> p a b", a=32, b=16)
        Bre, Bim, Cre, Cim = tb(0), tb(1), tb(2), tb(3)
        w = [self.av(12 * 1024 + i * 128, [32], F32) for i in range(16)]
        dt_, a_, b_, mag, u1, sn, cs, lre, lim, nre, den, wre, wim, x1, x2, x3 = w
        PI = float(np.pi)
        self.act(dt_, logdt, AF.Exp)
        self.tt('dve', a_, lamre, dt_, ALU.mult)
        self.tt('dve', b_, lamim, dt_, ALU.mult)
        self.act(mag, a_, AF.Exp)
        MAGIC = 12582912.0
        for dst, sh in ((sn, 0.0), (cs, 0.25)):
            self.ts('dve', u1, b_, 1.0 / (2 * PI), sh, ALU.mult, ALU.add)
            self.ts('dve', x1, u1, MAGIC, None, ALU.add)
            self.ts('dve', x1, x1, -MAGIC, None, ALU.add)
            self.tt('dve', u1, u1, x1, ALU.subtract)
            self.act(dst, u1, AF.Sin, scale=2 * PI)
        self.tt('dve', lre, mag, cs, ALU.mult)
        self.tt('dve', lim, mag, sn, ALU.mult)
        for c in range(2):
            self.cp('dve', self.lamA[:, l, c, :], lre)
        self.ts('dve', self.lamB[:, l, 0, :], lim, -1.0, None, ALU.mult)
        self.cp('dve', self.lamB[:, l, 1, :], lim)
        pr, pi_ = x1, x2
        self.cp('dve', pr, lre); self.cp('dve', pi_, lim)
        for _ in range(3):
            self.tt('dve', x3, pr, pr, ALU.mult)
            self.tt('dve', den, pi_, pi_, ALU.mult)
            self.tt('dve', pi_, pr, pi_, ALU.mult)
            self.ts('dve', pi_, pi_, 2.0, None, ALU.mult)
            self.tt('dve', pr, x3, den, ALU.subtract)
        for c in range(2):
            self.cp('dve', self.lam8A[:, l, c, :], pr)
        self.ts('dve', self.lam8B[:, l, 0, :], pi_, -1.0, None, ALU.mult)
        self.cp('dve', self.lam8B[:, l, 1, :], pi_)
        self.ts('dve', nre, lre, -1.0, None, ALU.add)
        self.tt('dve', x1, lamre, lamre, ALU.mult)
        self.tt('dve', x2, lamim, lamim, ALU.mult)
        self.tt('dve', den, x1, x2, ALU.add)
        self.recip(den, den)
        self.tt('dve', x1, nre, lamre, ALU.mult)
        self.tt('dve', x2, lim, lamim, ALU.mult)
        self.tt('dve', x3, x1, x2, ALU.add)
        self.tt('dve', wre, x3, den, ALU.mult)
        self.tt('dve', x1, lim, lamre, ALU.mult)
        self.tt('dve', x2, nre, lamim, ALU.mult)
        self.tt('dve', x3, x1, x2, ALU.subtract)
        self.tt('dve', wim, x3, den, ALU.mult)
        bb = [self.av(16 * 1024 + i * 2048, [32, 16], F32) for i in range(4)]
        bbre, bbim, y1, y2 = bb
        bc = lambda v: v.unsqueeze(2).broadcast_to([128, 32, 16])
        self.tt('dve', y1, Bre, bc(wre), ALU.mult)
        self.tt('dve', y2, Bim, bc(wim), ALU.mult)
        self.tt('dve', bbre, y1, y2, ALU.subtract)
        self.tt('dve', y1, Bim, bc(wre), ALU.mult)
        self.tt('dve', y2, Bre, bc(wim), ALU.mult)
        self.tt('dve', bbim, y1, y2, ALU.add)
        M1 = self.av(32 * 1024, [8, 128], F32)
        for c, src in ((0, bbre), (1, bbim)):
            for v in range(2):
                self.memset('dve', M1, 0.0)
                for pm in (v, v + 2):
                    for h in range(2):
                        col = 32 * pm + 16 * h
                        self.cp('dve', M1[h * 64:(h + 1) * 64, :, col:col + 16],
                                src[h * 64:(h + 1) * 64, pm::4, :])
                for ct in range(8):
                    base = ((c * 8 + ct) * 2 + v) * 128
                    self.tr32(self.buw[:, base:base + 128], M1[:, ct, :], 128, 128)
        for c, src, sg in ((0, Cre, 1.0), (1, Cim, -1.0)):
            base = c * 32 * 64
            cwv = self.cw[:, base: base + 32 * 64].rearrange("p (a b) -> p a b", a=32, b=64)
            self.memset('dve', cwv, 0.0)
            for pm2 in range(2):
                for h in range(2):
                    col = pm2 * 32 + h * 16
                    self.ts('dve', cwv[h * 64:(h + 1) * 64, pm2::2, col:col + 16],
                            src[h * 64:(h + 1) * 64, pm2::2, :], sg, None, ALU.mult)

        self.dma(self.s5w_scr[l][:, 0:4096], self.buw[:, :])
        self.dma(self.s5w_scr[l][:, 4096:8192], self.cw[:, :])

    def load_x(self, blk, sample):
        src = self.xs if sample else self.xp[blk * T:(blk + 1) * T, :]
        stg = self.av(0, [2, D], F32)
        for tt_ in range(2):
            self.dma(stg[:, tt_, :], src[tt_ * 128:(tt_ + 1) * 128, :])
        for tt_ in range(2):
            for k0 in range(0, 16, 4):
                bk = self.s.bank()
                items = [(bk[:, j * 128:(j + 1) * 128], stg[:, tt_, (k0 + j) * 128:(k0 + j + 1) * 128]) for j in range(4)]
                self.tr_multi(items, self.ident_f)
                self.cp('act' if (k0 // 4) % 2 else 'dve', self.xT[:, k0:k0 + 4, tt_ * 128:(tt_ + 1) * 128],
                        bk[:, :].rearrange("p (a b) -> p a b", a=4, b=128))

    def rmsnorm(self, gcols, out_fn):
        xsq = self.av(0, [16, T], BF16)
        self.act(xsq, self.xT[:, :, :], AF.Square)
        bk = self.s.bank()
        self.mm(bk[:, 0:T], [(self.ones_b[:, :], xsq[:, k, :]) for k in range(16)])
        rstd = self.av(8192, [T], F32)
        self.act(rstd, bk[:, 0:T], AF.Sqrt, bias=1e-6, scale=1.0 / D)
        self.recip(rstd, rstd)
        for k in range(16):
            self.stt('dve', out_fn(k), self.xT[:, k, :], gcols[:, k:k + 1], rstd, ALU.mult, ALU.mult)

    def final_out(self, blk, sample):
        yf = self.av(16 * 1024, [16, T], F32)
        self.rmsnorm(self.cvec[:, 0, O_FNG:O_FNG + 16], lambda k: yf[:, k, :])
        dst = self.ys if sample else self.yp[blk * T:(blk + 1) * T, :]
        stg = self.av(32 * 1024, [2, D], F32)
        for tt_ in range(2):
            for k0 in range(0, 16, 4):
                bk = self.s.bank()
                items = [(bk[:, j * 128:(j + 1) * 128], yf[:, k0 + j, tt_ * 128:(tt_ + 1) * 128]) for j in range(4)]
                self.tr_multi(items, self.ident_f)
                self.cp('act' if (k0 // 4) % 2 else 'dve', stg[:, tt_, k0 * 128:(k0 + 4) * 128], bk[:, :])
            self.dma(dst[tt_ * 128:(tt_ + 1) * 128, :], stg[:, tt_, :], is_output=True)

    def bg_gates(self, l):
        for m in range(3):
            for c0 in range(0, 2048, 256):
                col = 8192 + m * 2048 + c0
                w = self.load_w((l, 'in', col), self.w_in[l][:, col: col + 256], 2048, 256)
                for mt in range(2):
                    bk = self.s.bank()
                    o = bk[:, 0:T]
                    self.mm(o, [(w[:, k, mt * 128:(mt + 1) * 128], self.xnT[:, k, :]) for k in range(16)])
                    self.act(self.sg[m][:, c0 // 128 + mt, :], o, AF.Sigmoid)
                    self.bg_done += 1
                    yield

    def gen_v(self, l, vtok):
        for c0 in range(0, 1024, 256):
            w = self.load_w((l, 'in', 6144 + c0), self.w_in[l][:, 6144 + c0: 6144 + c0 + 256], 2048, 256)
            for tt_ in range(2):
                bk = self.s.bank()
                self.mm(bk[:, 0:256], [(self.xnT[:, k, tt_ * 128:(tt_ + 1) * 128], w[:, k, :]) for k in range(16)])
                self.cp('act', vtok[:, tt_, c0:c0 + 256], bk[:, 0:256])
                yield

    def gen_silu_proj(self, l, col0, dst):
        for c0 in range(0, 1024, 256):
            w = self.load_w((l, 'in', col0 + c0), self.w_in[l][:, col0 + c0: col0 + c0 + 256], 2048, 256)
            for mt in range(2):
                bk = self.s.bank()
                o = bk[:, 0:T]
                self.mm(o, [(w[:, k, mt * 128:(mt + 1) * 128], self.xnT[:, k, :]) for k in range(16)])
                self.act(dst[:, c0 // 128 + mt, :], o, AF.Silu)
                yield

    def bg_all(self, l):
        KB = 1024
        for _ in self.bg_gates(l):
            yield
        for g in (self.gen_silu_proj(l, 3072, self.av(36 * KB, [8, T], BF16)),
                  self.gen_v(l, self.av(40 * KB, [2, 1024], BF16)),
                  self.gen_silu_proj(l, 7168, self.av(44 * KB, [8, T], BF16))):
            for _ in g:
                self.bg_done += 1
                yield

    def drain(self, n):
        while self.bg is not None and self.bg_done < n:
            self.tick()

    def tick(self, n=1):
        for _ in range(n):
            if self.bg is None:
                if getattr(self, 'warm', False):
                    bk = self.s.bank()
                    self.mm(bk[:, 0:T], [(self.ident_b[:, :], self.xnT[:, k, :]) for k in range(8)])
                return
            try:
                next(self.bg)
            except StopIteration:
                self.bg = None

    def gate_branch(self, l, m, ysb):
        tmp = self.av(28 * 1024, [T], F32)
        tmp2 = self.av(29 * 1024, [T], F32)
        use_bg = self.sg is not None
        if use_bg:
            self.drain(16 * (m + 1))
        else:
            sgt = self.av(96 * 1024, [4, T], F32)
        for dc in range(4):
            if not use_bg:
                def ev_g(j, o):
                    self.act(sgt[:, j % 4, :], o, AF.Sigmoid)
                self.proj_fm(l, 8192 + m * 2048 + dc * 512, 512, ev_g)
            w = self.load_w((l, 'wb', m, dc), self.wb[m][l][:, dc * 512:(dc + 1) * 512], 1024, 512)

            def ev_b(j, o, dc=dc):
                dt_ = dc * 4 + j
                g = self.sg[m][:, dt_, :] if use_bg else sgt[:, j, :]
                if m == 0:
                    self.tt('dve', self.merged[:, dt_, :], g, o, ALU.mult)
                else:
                    tm = tmp if dt_ % 2 == 0 else tmp2
                    self.tt('dve', tm, g, o, ALU.mult)
                    self.tt('dve', self.merged[:, dt_, :], self.merged[:, dt_, :], tm, ALU.add)
            self.fm_chunk(w, 8, 4, lambda k: ysb[:, k, :], ev_b)

    def layer(self, l, blk, sample):
        self.dma(self.buw[:, :], self.s5w_scr[l][:, 0:4096])
        self.dma(self.cw[:, :], self.s5w_scr[l][:, 4096:8192])
        self.rmsnorm(self.cvec[:, l, O_NG:O_NG + 16], lambda k: self.xnT[:, k, :])
        ysb = self.av(72 * 1024, [8, T], BF16)
        self.sg = [self.av((80 + 8 * m) * 1024, [16, T], BF16) for m in range(3)]
        self.bg_done = 0
        self.bg = self.bg_all(l)
        self.s5(l, blk, sample, ysb)
        g_lru = self.lru(l, blk, sample, ysb)
        next(g_lru)
        self.gate_branch(l, 0, ysb)
        for _ in g_lru:
            pass
        g_ret = self.ret(l, blk, sample, ysb)
        next(g_ret)
        self.gate_branch(l, 1, ysb)
        for _ in g_ret:
            pass
        self.gate_branch(l, 2, ysb)
        for dc in range(8):
            w = self.load_w((l, 'out', dc), self.w_out[l][:, dc * 256:(dc + 1) * 256], 2048, 256)

            def ev_o(j, o, dc=dc):
                dt_ = dc * 2 + j
                self.tt('dve', self.xT[:, dt_, :], self.xT[:, dt_, :], o, ALU.add)
            self.fm_chunk(w, 16, 2, lambda k: self.merged[:, k, :], ev_o)

    def cstep(self, out, prev, add, A, Bsw, t1, t2):
        self.tt('dve', t1, prev, A, ALU.mult)
        self.tt('dve', t2, prev[:, :, ::-1, :], Bsw, ALU.mult)
        self.tt('dve', t1, t1, t2, ALU.add)
        self.tt('dve', out, add, t1, ALU.add)
        self.tick()

    def s5(self, l, blk, sample, ysb):
        cv = self.cvec[:, l, :]
        KB = 1024
        BUH = self.av(0, [128, 2, 32], F32)
        hbs = [self.av((32 + 2 * j) * KB, [128, 2, 4], BF16) for j in range(2)]
        uT = self.av(48 * KB, [8, T], BF16)
        sz = self.av(52 * KB, [8, T], BF16)
        yT = self.av(56 * KB, [8, T], F32)
        nsq, J = (16, 1) if sample else (1, 16)
        NJ = nsq * J
        t1 = self.av(64 * KB, [NJ, 2, 32], F32)
        t2 = self.av(68 * KB, [NJ, 2, 32], F32)
        G = self.av(72 * KB, [NJ, 2, 32], F32)
        Sall = self.av(76 * KB, [NJ, 2, 32], F32)
        self.proj_fm(l, 0, 1024, lambda j, o: self.cp('act', uT[:, j, :], o))
        self.proj_fm(l, 1024, 1024, lambda j, o: self.act(sz[:, j, :], o, AF.Silu))
        if sample:
            Zs = self.merged[:, :, :].rearrange("p a t -> p (a t)").bitcast(F32).rearrange(
                "p (s c q) -> p s c q", s=NSEQ, c=2, q=32)
        bcA = lambda tb, n: tb.unsqueeze(1).broadcast_to([128, n, 2, 32])
        A1, B1 = bcA(self.lamA[:, l, :, :], NJ), bcA(self.lamB[:, l, :, :], NJ)
        BUH4 = BUH.rearrange("p (n s) c q -> p n s c q", n=NJ, s=8)
        self.warm = True
        for sbk in range(2):
            tok0 = sbk * 128
            for pt0 in range(0, 32, 2):
                bk = self.s.bank()
                items = []
                for dp in range(2):
                    pt = pt0 + dp
                    ct, pm = pt // 4, pt % 4
                    hf, v = pm // 2, pm % 2
                    for c in range(2):
                        base = ((c * 8 + ct) * 2 + v) * 128
                        items.append((bk[:, (dp * 2 + c) * 128:(dp * 2 + c + 1) * 128],
                                      self.buw[hf * 64:(hf + 1) * 64, base:base + 128],
                                      uT[hf * 64:(hf + 1) * 64, ct, tok0:tok0 + 128]))
                self.mm_multi(items)
                dst = BUH[:, :, :, pt0:pt0 + 2].rearrange("p t c q -> p q c t")
                self.cp('act', dst, bk[:, :].rearrange("p (q c t) -> p q c t", q=2, c=2, t=128))
            if sample and sbk == 0:
                self.load_s5_state(l, Zs)
            for st in range(1, 8):
                prev = BUH4[:, :, 0, :, :] if st == 1 else G
                self.cstep(G, prev, BUH4[:, :, st, :, :], A1, B1, t1, t2)
                self.tick()
            if sample:
                Z0 = Zs[:, sbk * 16:(sbk + 1) * 16, :, :]
                self.cp('dve', Sall, Z0)
                self.cstep(Z0, Sall, G, bcA(self.lam8A[:, l, :, :], 16), bcA(self.lam8B[:, l, :, :], 16), t1, t2)
            else:
                Zst = self.Zp[:, l, :, :].unsqueeze(1)
                A8, B8 = bcA(self.lam8A[:, l, :, :], 1), bcA(self.lam8B[:, l, :, :], 1)
                self.cp('dve', Sall[:, 0:1, :, :], Zst)
                for j in range(16):
                    dstS = Sall[:, j + 1:j + 2, :, :] if j < 15 else Zst
                    self.cstep(dstS, Sall[:, j:j + 1, :, :], G[:, j:j + 1, :, :], A8, B8,
                               t1[:, 0:1, :, :], t2[:, 0:1, :, :])
            for st in range(8):
                prev = Sall if st == 0 else BUH4[:, :, st - 1, :, :]
                self.cstep(BUH4[:, :, st, :, :], prev, BUH4[:, :, st, :, :], A1, B1, t1, t2)
                self.tick()
            for ct in range(8):
                hb = hbs[ct % 2]
                self.cp('act', hb, BUH[:, :, :, 4 * ct:4 * ct + 4])
                bk = self.s.bank()
                for hf in range(2):
                    pairs = []
                    for pm2 in range(2):
                        pt = 4 * ct + 2 * hf + pm2
                        for c in range(2):
                            base = c * 32 * 64 + pt * 64
                            pairs.append((self.cw[:, base:base + 64], hb[:, :, c, pt % 4]))
                    self.mm(bk[hf * 64:(hf + 1) * 64, 0:128], pairs)
                self.stt('dve', yT[:, ct, tok0:tok0 + 128], uT[:, ct, tok0:tok0 + 128],
                         cv[:, O_D + ct:O_D + ct + 1], bk[:, 0:128], ALU.mult, ALU.add)
        self.warm = False
        if sample:
            self.store_s5_state(l, Zs)
        ygb = self.av(0, [8, T], BF16)
        sig = self.av(8 * KB, [T], F32)
        sig2 = self.av(9 * KB, [T], F32)
        gq = self.av(16 * KB, [8, T], F32)
        self.act(gq, yT, AF.Square)
        self.ts('dve', gq, gq, 0.044715, 1.0, ALU.mult, ALU.add)
        self.tt('dve', gq, gq, yT, ALU.mult)
        self.act(gq, gq, AF.Sigmoid, scale=1.5957691216057308)
        self.tt('dve', yT, yT, gq, ALU.mult)
        self.cp('dve', ygb, yT)
        for c0 in range(0, 1024, 512):
            w = self.load_w((l, 'glu', c0), self.w_glu[l][:, c0:c0 + 512], 1024, 512)

            def ev(j, o, c0=c0):
                ct = c0 // 128 + j
                sg_ = sig if ct % 2 == 0 else sig2
                self.act(sg_, o, AF.Sigmoid, bias=cv[:, O_BGLU + ct:O_BGLU + ct + 1])
                self.tt('dve', sg_, sg_, yT[:, ct, :], ALU.mult)
                self.tt('dve', ysb[:, ct, :], sg_, sz[:, ct, :], ALU.mult)
            self.fm_chunk(w, 8, 4, lambda k: ygb[:, k, :], ev)

    def load_s5_state(self, l, Zs):
        stg = self.av(64 * 1024, [2048], F32)
        for c, src in ((0, self.st_s5re), (1, self.st_s5im)):
            for hv in range(2):
                self.dma(stg[0:NSEQ, :], src[l][:, hv * 2048:(hv + 1) * 2048])
                bk = self.s.bank()
                self.tr_multi([(bk[:, q * NSEQ:(q + 1) * NSEQ], stg[0:NSEQ, q * 128:(q + 1) * 128]) for q in range(16)],
                              self.ident_f[0:NSEQ, 0:NSEQ])
                self.cp('act', Zs[:, :, c, hv * 16:(hv + 1) * 16].rearrange("p s q -> p q s"),
                        bk[:, :].rearrange("p (q s) -> p q s", q=16, s=NSEQ))

    def store_s5_state(self, l, Zs):
        stg = self.av(64 * 1024, [2048], F32)
        for c in range(2):
            for hv in range(2):
                for q4 in range(4):
                    bk = self.s.bank()
                    self.tr_multi([(bk[0:NSEQ, j * 128:(j + 1) * 128], Zs[:, :, c, hv * 16 + q4 * 4 + j]) for j in range(4)],
                                  self.ident_f)
                    self.cp('act', stg[0:NSEQ, q4 * 512:(q4 + 1) * 512], bk[0:NSEQ, :])
                self.dma(self.o_s5_s[c][l][:, hv * 2048:(hv + 1) * 2048], stg[0:NSEQ, :], is_output=True)

    def prompt_state_out(self):
        stg = self.av(0, [2, 2, 128], F32)
        for l in range(2):
            for c in range(2):
                self.tr32(stg[0:32, l, c, :], self.Zp[:, l, c, :], 128, 32, evac='act')
                self.dma(self.o_s5_p[c][l].rearrange("(a b) -> a b", b=128), stg[0:32, l, c, :], is_output=True)
        stg2 = self.av(8192, [2, 128], F32)
        for l in range(2):
            self.tr32(stg2[0:8, l, :], self.hst[:, l, :], 128, 8, evac='act')
            self.dma(self.o_lru_p[l].rearrange("(a b) -> a b", b=128), stg2[0:8, l, :], is_output=True)
        stg3 = self.av(12288, [2, 3, 128], F32)
        for l in range(2):
            for k in range(3):
                self.tr32(stg3[0:8, l, k, :], self.convc[:, l, :, k], 128, 8, evac='act')
                self.dma(self.o_conv_p[l, k].rearrange("(a b) -> a b", b=128), stg3[0:8, l, k, :], is_output=True)
        for l in range(2):
            self.dma(self.o_ret_p[l].rearrange("h d e -> d h e"), self.Rp[:, l, :, :], is_output=True)

    def lru(self, l, blk, sample, ysb):
        cv = self.cvec[:, l, :]
        KB = 1024
        nseq, L = (NSEQ, 8) if sample else (1, T)
        xp_ = self.av(24 * KB, [8, nseq, 3 + L], F32)
        xc = self.av(64 * KB, [8, nseq, L], F32)
        xcb = self.av(76 * KB, [8, T], BF16)
        r_all = self.av(24 * KB, [8, nseq, L], F32)
        szl = self.av(36 * KB, [8, T], BF16)
        i_all = self.av(56 * KB, [8, nseq, L], F32)
        a_all = self.av(8 * KB, [8, nseq, L], F32)
        m_all = self.av(80 * KB, [8, nseq, L], F32)
        hl = self.av(20 * KB, [8, NSEQ], F32)
        t8 = self.av(21 * KB, [8, NSEQ], F32)
        stg = self.av(16 * KB, [1024], F32)
        v3 = lambda o: o.rearrange("p (s t) -> p s t", s=nseq, t=L)
        self.proj_fm(l, 2048, 1024, lambda j, o: self.cp('act', xp_[:, j, :, 3:3 + L], v3(o)))
        if self.sg is None:
            self.proj_fm(l, 3072, 1024, lambda j, o: self.act(szl[:, j, :], o, AF.Silu))
        else:
            self.drain(56)
        if sample:
            for k in range(3):
                self.dma(stg[0:NSEQ, :], self.st_conv[l, :, k, :])
                for ct in range(8):
                    self.tr32(xp_[:, ct, :, k], stg[0:NSEQ, ct * 128:(ct + 1) * 128], NSEQ, 128, evac='act')
            self.dma(stg[0:NSEQ, :], self.st_lru[l])
            for ct in range(8):
                self.tr32(hl[:, ct, :], stg[0:NSEQ, ct * 128:(ct + 1) * 128], NSEQ, 128, evac='act')
        else:
            self.cp('dve', xp_[:, :, 0, 0:3], self.convc[:, l, :, :])
        for ct in range(8):
            cw = lambda k: cv[:, O_CW + k * 8 + ct: O_CW + k * 8 + ct + 1]
            self.act(xc[:, ct, :, :], xp_[:, ct, :, 0:L], AF.Identity, bias=cv[:, O_CB + ct:O_CB + ct + 1], scale=cw(0))
            for k in range(1, 4):
                self.stt('dve', xc[:, ct, :, :], xp_[:, ct, :, k:k + L], cw(k), xc[:, ct, :, :], ALU.mult, ALU.add)
            self.cp('act', xcb[:, ct, :], xc[:, ct, :, :].rearrange("p s t -> p (s t)"))
        if sample:
            for k in range(3):
                for ct in range(8):
                    self.tr32(stg[0:NSEQ, ct * 128:(ct + 1) * 128], xp_[:, ct, :, L + k], 128, NSEQ, evac='act')
                self.dma(self.o_conv_s[l, :, k, :], stg[0:NSEQ, :], is_output=True)
        else:
            self.cp('dve', self.convc[:, l, :, :], xp_[:, :, 0, L:L + 3])
        yield
        wa = self.load_w((l, 'wa'), self.w_a[l], 1024, 128)
        wx = self.load_w((l, 'wx'), self.w_x[l], 1024, 128)
        f3 = lambda v: v.rearrange("p a s t -> p a (s t)")
        f2 = lambda v: v.rearrange("p a s t -> p (a s t)")
        for n in range(8):
            bk = self.s.bank()
            self.mm(bk[:, 0:T], [(wa[:, n, :], xcb[:, n, :])])
            self.act(f3(r_all)[:, n, :], bk[:, 0:T], AF.Sigmoid, bias=cv[:, O_BA + n:O_BA + n + 1])
            bk2 = self.s.bank()
            self.mm(bk2[:, 0:T], [(wx[:, n, :], xcb[:, n, :])])
            self.act(f3(i_all)[:, n, :], bk2[:, 0:T], AF.Sigmoid, bias=cv[:, O_BX + n:O_BX + n + 1])
        self.tt('dve', f3(a_all), f3(r_all), self.c1[:, l, :].unsqueeze(2).broadcast_to([128, 8, T]), ALU.mult)
        self.act(f2(a_all), f2(a_all), AF.Exp)
        self.act(f2(m_all), f2(a_all), AF.Square)
        self.act(f2(m_all), f2(m_all), AF.Sqrt, bias=1.0, scale=-1.0)
        self.tt('dve', f2(m_all), f2(m_all), f2(i_all), ALU.mult)
        self.tt('dve', f2(m_all), f2(m_all), f2(xc), ALU.mult)
        if sample:
            self.tt('dve', t8, a_all[:, :, :, 0], hl, ALU.mult)
            self.tt('dve', m_all[:, :, :, 0], m_all[:, :, :, 0], t8, ALU.add)
        else:
            self.tt('dve', t8[:, :, 0:1], a_all[:, :, :, 0], self.hst[:, l, :].unsqueeze(2), ALU.mult)
            self.tt('dve', m_all[:, :, :, 0], m_all[:, :, :, 0], t8[:, :, 0:1], ALU.add)
        self.memset('dve', a_all[:, :, :, 0], 0.0)
        hT = r_all
        self.scan(f2(hT), f2(a_all), f2(m_all))
        if sample:
            self.cp('dve', hl, hT[:, :, :, L - 1])
        else:
            self.cp('dve', self.hst[:, l, :].unsqueeze(2), hT[:, :, :, L - 1])
        self.tt('dve', ysb.rearrange("p a t -> p (a t)"), f2(hT), szl.rearrange("p a t -> p (a t)"), ALU.mult)
        if sample:
            for ct in range(8):
                self.tr32(stg[0:NSEQ, ct * 128:(ct + 1) * 128], hl[:, ct, :], 128, NSEQ, evac='act')
            self.dma(self.o_lru_s[l], stg[0:NSEQ, :], is_output=True)

    def ret(self, l, blk, sample, ysb):
        cv = self.cvec[:, l, :]
        KB = 1024
        grp = 1 if sample else 0
        rt = self.av(0, [4, T], F32)
        rset = [[self.av((4 + j) * KB, [T], F32) for j in range(3)], [self.av((32 + j) * KB, [T], F32) for j in range(3)]]
        qr = self.av(8 * KB, [8, T], BF16); kr = self.av(12 * KB, [8, T], BF16); qxi = self.av(16 * KB, [8, T], BF16)
        oT = self.av(20 * KB, [8, T], F32)
        kzs = [self.av(28 * KB + 256 * j, [128], BF16) for j in range(2)]
        Sms = [self.av(28 * KB + 512 + 256 * j, [128], BF16) for j in range(2)]
        gset = [[self.av((29 + j) * KB, [T], F32) for j in range(3)], [self.av((56 + j) * KB, [T], F32) for j in range(3)]]
        vtok = self.av(40 * KB, [2, 1024], BF16); szr = self.av(44 * KB, [8, T], BF16)
        msk = self.av(48 * KB, [2, 8, 128], F32)
        R0f = self.av(60 * KB, [16, 128], F32); R0b = self.av(68 * KB, [16, 128], BF16)
        Vblk = self.av(76 * KB, [16, 128], BF16); Rn = self.av(80 * KB, [16, 128], F32)
        self.dma(rt, self.rope_s if sample else self.rope_p[:, :, blk * T:(blk + 1) * T])
        self.dma(msk, self.rmask_d[grp])
        zeta = self.rzeta[:, grp, :]
        xiv = lambda h: msk[:, 1, h, :].unsqueeze(1).broadcast_to([128, 2, 128])

        def rope(o, h, dst, ci, si, with_xi):
            qf, t1, t3 = rset[h % 2]
            self.cp('act', qf, o)
            bk = self.s.bank()
            self.mm(bk[:, 0:T], [(self.swap_f, qf)])
            self.tt('dve', t1, qf, rt[:, ci, :], ALU.mult)
            self.tt('dve', t3, bk[:, 0:T], rt[:, si, :], ALU.mult)
            self.tt('dve', t3, t3, t1, ALU.add)
            self.cp('dve', dst[:, h, :], t3)
            if with_xi:
                self.tt('dve', qxi[:, h, :].rearrange("p (a b) -> p a b", a=2, b=128),
                        t3.rearrange("p (a b) -> p a b", a=2, b=128), xiv(h), ALU.mult)
        self.proj_fm(l, 4096, 1024, lambda j, o: rope(o, j, qr, 0, 1, True))
        self.proj_fm(l, 5120, 1024, lambda j, o: rope(o, j, kr, 2, 3, False))
        if self.sg is None:
            for _ in self.gen_v(l, vtok):
                pass
            self.proj_fm(l, 7168, 1024, lambda j, o: self.act(szr[:, j, :], o, AF.Silu))
        else:
            self.drain(72)
        yield
        for ck in range(2):
            for h in range(8):
                kz, Sm = kzs[h % 2], Sms[h % 2]
                tk = slice(ck * 128, (ck + 1) * 128)
                vh = vtok[:, ck, h * 128:(h + 1) * 128]
                bk = self.s.bank()
                kb = bk[:, 0:64].bitcast(BF16)
                self.tr_multi([(kb, kr[:, h, tk])], self.ident_b[:, :])
                self.act(kz, kb, AF.Identity, scale=zeta[:, h:h + 1])
                bk1 = self.s.bank()
                self.mm(bk1[:, 0:128], [(kr[:, h, tk], qr[:, h, tk])])
                self.tt('dve', Sm, bk1[:, 0:128], msk[:, 0, h, :], ALU.mult)
                if not sample:
                    bk2 = self.s.bank()
                    self.mm(bk2[:, 0:128], [(vh, Sm), (self.Rpb[:, l, h, :], qxi[:, h, tk])])
                    self.cp('act', oT[:, h, tk], bk2[:, 0:128])
                    bk3 = self.s.bank()
                    self.mm(bk3[:, 0:128], [(kz, vh)])
                    self.stt('dve', self.Rp[:, l, h, :], self.Rp[:, l, h, :], RET_G[h] ** 128, bk3[:, 0:128],
                             ALU.mult, ALU.add)
                    self.cp('dve', self.Rpb[:, l, h, :], self.Rp[:, l, h, :])
                else:
                    sq0 = ck * 16
                    self.dma(R0f, self.st_ret[l, sq0:sq0 + 16, h].rearrange("s d e -> d s e"))
                    self.cp('dve', R0b, R0f)
                    bk2 = self.s.bank()
                    self.mm(bk2[:, 0:128], [(vh, Sm)])
                    self.cp('act', oT[:, h, tk], bk2[:, 0:128])
                    bk4 = self.s.bank()
                    self.mm_multi([(bk4[:, s * 8:(s + 1) * 8], R0b[:, s, :], qxi[:, h, ck * 128 + s * 8: ck * 128 + (s + 1) * 8])
                                   for s in range(16)])
                    self.tt('dve', oT[:, h, tk], oT[:, h, tk], bk4[:, 0:128], ALU.add)
                    self.tt('dve', Vblk, vh.unsqueeze(1).broadcast_to([128, 16, 128]),
                            self.bm[:, :].unsqueeze(2).broadcast_to([128, 16, 128]), ALU.mult)
                    for q4 in range(4):
                        bk3 = self.s.bank()
                        self.mm(bk3[:, :], [(kz, Vblk[:, q4 * 4:(q4 + 1) * 4, :].rearrange("p a b -> p (a b)"))])
                        self.stt('dve', Rn[:, q4 * 4:(q4 + 1) * 4, :], R0f[:, q4 * 4:(q4 + 1) * 4, :], RET_G[h] ** 8,
                                 bk3[:, :].rearrange("p (a b) -> p a b", a=4, b=128), ALU.mult, ALU.add)
                    self.dma(self.o_ret_s[l, sq0:sq0 + 16, h].rearrange("s d e -> d s e"), Rn, is_output=True)
        for h in range(8):
            g1, g2, g3 = gset[h % 2]
            o_h = oT[:, h, :]
            bkm = self.s.bank()
            self.mm(bkm[:, 0:T], [(self.ones_f, o_h)])
            self.act(g1, o_h, AF.Square)
            bkv = self.s.bank()
            self.mm(bkv[:, 0:T], [(self.ones_f, g1)])
            self.act(g2, bkm[:, 0:T], AF.Identity, scale=1.0 / 128)
            self.tt('dve', g3, g2, g2, ALU.mult)
            self.stt('dve', g3, bkv[:, 0:T], 1.0 / 128, g3, ALU.mult, ALU.subtract)
            self.act(g3, g3, AF.Sqrt, bias=1e-5)
            self.recip(g3, g3)
            self.tt('dve', g1, o_h, g2, ALU.subtract)
            self.tt('dve', g1, g1, g3, ALU.mult)
            self.stt('dve', ysb[:, h, :], g1, cv[:, O_GN + h:O_GN + h + 1], szr[:, h, :], ALU.mult, ALU.mult)


_NC = None


def _get_nc():
    global _NC
    if _NC is None:
        _NC = B().nc
    return _NC


def _vec(v, n):
    return np.ascontiguousarray(v.reshape(n, 128).T)


def _consts():
    c = {}
    ident = np.eye(128, dtype=np.float32)
    ones = np.ones((128, 128), np.float32)
    swap = np.zeros((128, 128), np.float32)
    for m in range(128):
        swap[(m + 64) % 128, m] = 1.0
    c['cmat'] = np.ascontiguousarray(np.stack([ident, ones, swap], axis=1))

    def rope(pos):
        half = 64
        freq = (np.float32(10000.0) ** (-np.arange(half, dtype=np.float32) / np.float32(half))).astype(np.float32)
        ang = (pos.astype(np.float32)[:, None] * freq[None, :]).astype(np.float32)
        cos = np.cos(ang).astype(np.float32).T; sin = np.sin(ang).astype(np.float32).T
        cos2 = np.concatenate([cos, cos], 0); sinS = np.concatenate([-sin, sin], 0)
        sc = np.float32(128.0 ** -0.5)
        return np.ascontiguousarray(np.stack([cos2, sinS, cos2 * sc, sinS * sc], axis=1).astype(np.float32))
    c['rope_p'] = rope(np.arange(2048, dtype=np.float32))
    c['rope_s'] = rope(np.tile(np.arange(8, dtype=np.float32) + 16384.0, 32))
    log_g = np.log1p(-np.exp2(-5.0 - np.arange(8, dtype=np.float64)))
    rmask = np.zeros((2, 128, 2, 8, 128), np.float32)
    rzeta = np.zeros((2, 128, 8), np.float32)
    j = np.arange(128)[:, None]; i = np.arange(128)[None, :]
    for h in range(8):
        dm = np.where(i >= j, np.exp((i - j).clip(0) * log_g[h]), 0.0)
        rmask[0, :, 0, h, :] = dm
        rmask[0, :, 1, h, :] = np.exp((i + 1) * log_g[h])
        rzeta[0, :, h] = np.exp((127 - np.arange(128)) * log_g[h])
        same = (i // 8) == (j // 8)
        dms = np.where(same & (i >= j), np.exp((i - j).clip(0) * log_g[h]), 0.0)
        rmask[1, :, 0, h, :] = dms
        rmask[1, :, 1, h, :] = np.exp(((i % 8) + 1) * log_g[h])
        rzeta[1, :, h] = np.exp((7 - np.arange(128) % 8) * log_g[h])
    c['rmask'] = rmask; c['rzeta'] = rzeta
    bm = np.zeros((128, 16), np.float32)
    bm[np.arange(128), np.arange(128) // 8] = 1.0
    c['bm'] = bm
    return c


def kernel(**inp):
    f = lambda k: np.asarray(inp[k], dtype=np.float32)
    shared = _consts()
    shared['w_in'] = f('w_in'); shared['w_glu'] = f('s5_w_glu')
    shared['w_a'] = np.ascontiguousarray(f('lru_w_a').reshape(2, 1024, 128))
    shared['w_x'] = np.ascontiguousarray(f('lru_w_x').reshape(2, 1024, 128))
    shared['wb_s5'] = f('w_branch_s5'); shared['wb_lru'] = f('w_branch_lru'); shared['wb_ret'] = f('w_branch_ret')
    shared['w_out'] = f('w_out')
    cvec = np.zeros((2, 128, NCV), np.float32)
    s5tab = np.zeros((2, 128, S5TW), np.float32)
    for l in range(2):
        cvec[l, :, O_NG:O_NG + 16] = _vec(f('norm_g')[l], 16)
        cvec[l, :, O_D:O_D + 8] = _vec(f('s5_d')[l], 8)
        cvec[l, :, O_BGLU:O_BGLU + 8] = _vec(f('s5_b_glu')[l], 8)
        for k in range(4):
            cvec[l, :, O_CW + 8 * k:O_CW + 8 * k + 8] = _vec(f('lru_conv_w')[l, k], 8)
        cvec[l, :, O_CB:O_CB + 8] = _vec(f('lru_conv_b')[l], 8)
        cvec[l, :, O_BA:O_BA + 8] = _vec(f('lru_b_a')[l], 8)
        cvec[l, :, O_BX:O_BX + 8] = _vec(f('lru_b_x')[l], 8)
        cvec[l, :, O_LAM:O_LAM + 8] = _vec(f('lru_lambda')[l], 8)
        cvec[l, :, O_GN:O_GN + 8] = _vec(f('ret_gn_g')[l], 8)
        cvec[l, :, O_FNG:O_FNG + 16] = _vec(f('final_norm_g'), 16)
        pl = lambda a: np.ascontiguousarray(a.reshape(32, 2, 64).transpose(1, 2, 0).reshape(128, 32))
        s5tab[l, :, 0:32] = pl(f('s5_lambda_re')[l])
        s5tab[l, :, 32:64] = pl(f('s5_lambda_im')[l])
        s5tab[l, :, 64:96] = pl(np.repeat(f('s5_log_dt')[l][:, None], 64, axis=1))
        pl3 = lambda a: np.ascontiguousarray(a.reshape(32, 2, 64, 16).transpose(1, 2, 0, 3).reshape(128, 512))
        s5tab[l, :, 96:608] = pl3(f('s5_b_re')[l])
        s5tab[l, :, 608:1120] = pl3(f('s5_b_im')[l])
        s5tab[l, :, 1120:1632] = pl3(f('s5_c_re')[l].transpose(0, 2, 1))
        s5tab[l, :, 1632:2144] = pl3(f('s5_c_im')[l].transpose(0, 2, 1))
    shared['cvec'] = cvec; shared['s5tab'] = s5tab
    in_maps = []
    for c in range(8):
        b = c % 4
        m = dict(shared)
        m['xp'] = np.ascontiguousarray(f('x_prompt')[b])
        sl = slice(32 * b, 32 * b + 32)
        m['xs'] = np.ascontiguousarray(f('x_sample')[sl].reshape(256, D))
        m['st_s5re'] = np.ascontiguousarray(f('state_s5_re')[:, sl].reshape(2, 32, 4096))
        m['st_s5im'] = np.ascontiguousarray(f('state_s5_im')[:, sl].reshape(2, 32, 4096))
        m['st_lru'] = np.ascontiguousarray(f('state_lru')[:, sl])
        m['st_conv'] = np.ascontiguousarray(f('state_conv')[:, sl])
        m['st_ret'] = np.ascontiguousarray(f('state_ret')[:, sl])
        in_maps.append(m)
    nc = _get_nc()
    res = run_bass_kernel_spmd(nc, in_maps, core_ids=list(range(8)))
    R = res.results[:4]
    cat = lambda k, ax: np.concatenate([r[k] for r in R], axis=ax)
    y_p = np.stack([r['yp'] for r in R], 0)
    y_s = cat('ys', 0).reshape(128, 8, D)
    pst = lambda k, shp: np.stack([r[k] for r in R], 1).reshape(shp)
    outs = (y_p, y_s,
            pst('o_s5re_p', (2, 4, 64, 64)), pst('o_s5im_p', (2, 4, 64, 64)),
            pst('o_lru_p', (2, 4, 1024)), pst('o_conv_p', (2, 4, 3, 1024)), pst('o_ret_p', (2, 4, 8, 128, 128)),
            cat('o_s5re_s', 1).reshape(2, 128, 64, 64), cat('o_s5im_s', 1).reshape(2, 128, 64, 64),
            cat('o_lru_s', 1), cat('o_conv_s', 1), cat('o_ret_s', 1))
    return tuple(np.ascontiguousarray(o, dtype=np.float32) for o in outs)
```
